# Optimizing a Trainium2 kernel written in Bass

```python
import jax, jax.numpy as jnp
from jax import lax
import numpy as np

D_MODEL = 1024
BATCH = 8
SEQ = 8192
DEPTH = 2
DEC_BATCH = 8
DEC_SEQ = 16
PAST_LEN = 1024

CHUNK = 64
N_MIXERS = 2
N_SSD = (DEPTH + 1) // 2
N_CONV = DEPTH // 2
D_FF = 2816
SSD_EXPAND = 2
D_INNER = SSD_EXPAND * D_MODEL
SSD_HEAD_DIM = 64
SSD_HEADS = D_INNER // SSD_HEAD_DIM
SSD_GROUPS = 4
SSD_STATE = 128
SSD_CONV_W = 4
SSD_CONV_DIM = D_INNER + 2 * SSD_GROUPS * SSD_STATE
SSD_PROJ = 2 * D_INNER + 2 * SSD_GROUPS * SSD_STATE + SSD_HEADS
D_CONV = D_MODEL
CONV_W = 31
N_MOD = 9
EPS = 1e-6

kernel_name = 'hybrid_ssd_conformer_stream_step'


def rms_norm(x, g):
    xf = x.astype(jnp.float32)
    y = xf * lax.rsqrt(jnp.mean(xf * xf, axis=-1, keepdims=True) + EPS)
    return (y * g.astype(jnp.float32)).astype(x.dtype)


def layer_norm(x, g, b):
    xf = x.astype(jnp.float32)
    mu = jnp.mean(xf, axis=-1, keepdims=True)
    xc = xf - mu
    y = xc * lax.rsqrt(jnp.mean(xc * xc, axis=-1, keepdims=True) + EPS)
    return (y * g.astype(jnp.float32) + b.astype(jnp.float32)).astype(x.dtype)


def swiglu(h, w1, w3, w2):
    return (jax.nn.silu(h @ w1) * (h @ w3)) @ w2


def causal_dwconv(x, buf, w, bias):
    k = w.shape[0]
    xp = jnp.concatenate([buf.astype(x.dtype), x], axis=1)
    y = lax.conv_general_dilated(
        xp, w[:, None, :].astype(x.dtype), window_strides=(1,), padding='VALID',
        dimension_numbers=('NWC', 'WIO', 'NWC'), feature_group_count=x.shape[-1])
    new_buf = xp[:, xp.shape[1] - (k - 1):]
    return y + bias.astype(x.dtype), new_buf


def ssd_chunked(x, dt, a, bm, cm, h0, chunk):
    b, seq, nh, p = x.shape
    g, n = bm.shape[-2:]
    hg = nh // g
    nc = seq // chunk
    xc = jnp.moveaxis(x.astype(jnp.float32).reshape(b, nc, chunk, g, hg, p), 1, 0)
    dtc = jnp.moveaxis(dt.reshape(b, nc, chunk, g, hg), 1, 0)
    bc = jnp.moveaxis(bm.astype(jnp.float32).reshape(b, nc, chunk, g, n), 1, 0)
    cc = jnp.moveaxis(cm.astype(jnp.float32).reshape(b, nc, chunk, g, n), 1, 0)
    a_g = a.reshape(g, hg)
    mask = jnp.tril(jnp.ones((chunk, chunk), dtype=bool))[None, :, :, None, None]

    def step(h, inp):
        xk, dtk, bk, ck = inp
        acum = jnp.cumsum(dtk * a_g, axis=1)
        seg = acum[:, :, None] - acum[:, None, :]
        decay = jnp.exp(jnp.where(mask, seg, -jnp.inf))
        scores = jnp.einsum('blgn,bsgn->blsg', ck, bk)
        wts = scores[..., None] * decay * dtk[:, None]
        y = jnp.einsum('blsgh,bsghp->blghp', wts, xk)
        y = y + jnp.einsum('blgn,bghpn->blghp', ck, h) * jnp.exp(acum)[..., None]
        a_last = acum[:, -1]
        w_end = jnp.exp(a_last[:, None] - acum) * dtk
        h_new = jnp.exp(a_last)[..., None, None] * h + jnp.einsum('blgh,blgn,blghp->bghpn', w_end, bk, xk)
        return h_new, y

    h_init = h0.astype(jnp.float32).reshape(b, g, hg, p, n)
    h_fin, ys = lax.scan(step, h_init, (xc, dtc, bc, cc))
    y = jnp.moveaxis(ys, 0, 1).reshape(b, seq, nh, p)
    return y, h_fin.reshape(b, nh, p, n)


def ssd_mixer(h, conv_buf, ssm_state, w_in, conv_w, conv_b, dt_bias, a_log, d_skip, norm_g, w_out):
    b, seq, _ = h.shape
    proj = h @ w_in
    z, xbc, dt_raw = jnp.split(proj, [D_INNER, D_INNER + SSD_CONV_DIM], axis=-1)
    xbc, new_buf = causal_dwconv(xbc, conv_buf, conv_w, conv_b)
    xbc = jax.nn.silu(xbc)
    xs, bm, cm = jnp.split(xbc, [D_INNER, D_INNER + SSD_GROUPS * SSD_STATE], axis=-1)
    dt = jax.nn.softplus(dt_raw.astype(jnp.float32) + dt_bias.astype(jnp.float32))
    a = -jnp.exp(a_log.astype(jnp.float32))
    xh = xs.reshape(b, seq, SSD_HEADS, SSD_HEAD_DIM)
    chunk = min(CHUNK, seq)
    y, new_state = ssd_chunked(xh, dt, a,
                               bm.reshape(b, seq, SSD_GROUPS, SSD_STATE),
                               cm.reshape(b, seq, SSD_GROUPS, SSD_STATE), ssm_state, chunk)
    y = y + d_skip.astype(jnp.float32)[:, None] * xh.astype(jnp.float32)
    y = y.reshape(b, seq, D_INNER).astype(h.dtype)
    y = rms_norm(y * jax.nn.silu(z), norm_g)
    return y @ w_out, new_buf, new_state.astype(h.dtype)


def conv_module(h, buf, w_pw1, b_pw1, dw_w, dw_b, ln_g, ln_b, w_pw2, b_pw2):
    u = h @ w_pw1 + b_pw1
    ua, ug = jnp.split(u, 2, axis=-1)
    u = ua * jax.nn.sigmoid(ug)
    u, new_buf = causal_dwconv(u, buf, dw_w, dw_b)
    u = jax.nn.silu(layer_norm(u, ln_g, ln_b))
    return u @ w_pw2 + b_pw2, new_buf


def trunk(x, c, ssd_state, ssd_conv_cache, cmod_conv_cache,
          mod_w, mod_b, norm_g, ffn_w1, ffn_w3, ffn_w2,
          ssd_w_in, ssd_conv_w, ssd_conv_b, ssd_dt_bias, ssd_a_log, ssd_d, ssd_norm_g, ssd_w_out,
          cmod_w_pw1, cmod_b_pw1, cmod_dw_w, cmod_dw_b, cmod_ln_g, cmod_ln_b, cmod_w_pw2, cmod_b_pw2,
          final_g):
    b = x.shape[0]
    new_ssd_state, new_ssd_conv, new_cmod_conv = [], [], []
    c_act = jax.nn.silu(c)
    for i in range(DEPTH):
        mod = (c_act @ mod_w[i] + mod_b[i]).reshape(b, N_MOD, 1, D_MODEL)
        sh1, sc1, gt1, sh2, sc2, gt2, sh3, sc3, gt3 = [mod[:, k] for k in range(N_MOD)]
        h = rms_norm(x, norm_g[i, 0]) * (1 + sc1) + sh1
        x = x + 0.5 * gt1 * swiglu(h, ffn_w1[i, 0], ffn_w3[i, 0], ffn_w2[i, 0])
        h = rms_norm(x, norm_g[i, 1]) * (1 + sc2) + sh2
        j = i // N_MIXERS
        if i % N_MIXERS == 0:
            out, nb, ns = ssd_mixer(h, ssd_conv_cache[j], ssd_state[j], ssd_w_in[j], ssd_conv_w[j],
                                    ssd_conv_b[j], ssd_dt_bias[j], ssd_a_log[j], ssd_d[j],
                                    ssd_norm_g[j], ssd_w_out[j])
            new_ssd_conv.append(nb)
            new_ssd_state.append(ns)
        else:
            out, nb = conv_module(h, cmod_conv_cache[j], cmod_w_pw1[j], cmod_b_pw1[j], cmod_dw_w[j],
                                  cmod_dw_b[j], cmod_ln_g[j], cmod_ln_b[j], cmod_w_pw2[j], cmod_b_pw2[j])
            new_cmod_conv.append(nb)
        x = x + gt2 * out
        h = rms_norm(x, norm_g[i, 2]) * (1 + sc3) + sh3
        x = x + 0.5 * gt3 * swiglu(h, ffn_w1[i, 1], ffn_w3[i, 1], ffn_w2[i, 1])
    return rms_norm(x, final_g), jnp.stack(new_ssd_state), jnp.stack(new_ssd_conv), jnp.stack(new_cmod_conv)


def setup_inputs(seed: int = 0) -> dict:
    key = jax.random.key(seed)
    ks = iter(jax.random.split(key, 48))

    def nrm(shape, scale):
        return scale * jax.random.normal(next(ks), shape, jnp.float32)

    dt0 = jnp.exp(jax.random.uniform(next(ks), (N_SSD, SSD_HEADS), jnp.float32,
                                     minval=float(np.log(1e-3)), maxval=float(np.log(1e-1))))
    ssd_dt_bias = dt0 + jnp.log(-jnp.expm1(-dt0))
    ssd_a_log = jnp.log(jax.random.uniform(next(ks), (N_SSD, SSD_HEADS), jnp.float32, minval=1.0, maxval=16.0))
    return {
        'x_prompt': nrm((BATCH, SEQ, D_MODEL), 1.0),
        'x_sample': nrm((DEC_BATCH, DEC_SEQ, D_MODEL), 1.0),
        'c_prompt': nrm((BATCH, D_MODEL), 1.0),
        'c_sample': nrm((DEC_BATCH, D_MODEL), 1.0),
        'state_ssd': nrm((N_SSD, DEC_BATCH, SSD_HEADS, SSD_HEAD_DIM, SSD_STATE), 0.5),
        'cache_ssd_conv': nrm((N_SSD, DEC_BATCH, SSD_CONV_W - 1, SSD_CONV_DIM), 1.0),
        'cache_cmod_conv': nrm((N_CONV, DEC_BATCH, CONV_W - 1, D_CONV), 1.0),
        'mod_w': nrm((DEPTH, D_MODEL, N_MOD * D_MODEL), 0.5 * D_MODEL ** -0.5),
        'mod_b': nrm((DEPTH, N_MOD * D_MODEL), 0.02),
        'norm_g': 1.0 + nrm((DEPTH, 3, D_MODEL), 0.05),
        'ffn_w1': nrm((DEPTH, 2, D_MODEL, D_FF), D_MODEL ** -0.5),
        'ffn_w3': nrm((DEPTH, 2, D_MODEL, D_FF), D_MODEL ** -0.5),
        'ffn_w2': nrm((DEPTH, 2, D_FF, D_MODEL), D_FF ** -0.5),
        'ssd_w_in': nrm((N_SSD, D_MODEL, SSD_PROJ), D_MODEL ** -0.5),
        'ssd_conv_w': nrm((N_SSD, SSD_CONV_W, SSD_CONV_DIM), SSD_CONV_W ** -0.5),
        'ssd_conv_b': nrm((N_SSD, SSD_CONV_DIM), 0.02),
        'ssd_dt_bias': ssd_dt_bias,
        'ssd_a_log': ssd_a_log,
        'ssd_d': 1.0 + nrm((N_SSD, SSD_HEADS), 0.1),
        'ssd_norm_g': 1.0 + nrm((N_SSD, D_INNER), 0.05),
        'ssd_w_out': nrm((N_SSD, D_INNER, D_MODEL), D_INNER ** -0.5),
        'cmod_w_pw1': nrm((N_CONV, D_MODEL, 2 * D_CONV), D_MODEL ** -0.5),
        'cmod_b_pw1': nrm((N_CONV, 2 * D_CONV), 0.02),
        'cmod_dw_w': nrm((N_CONV, CONV_W, D_CONV), CONV_W ** -0.5),
        'cmod_dw_b': nrm((N_CONV, D_CONV), 0.02),
        'cmod_ln_g': 1.0 + nrm((N_CONV, D_CONV), 0.05),
        'cmod_ln_b': nrm((N_CONV, D_CONV), 0.02),
        'cmod_w_pw2': nrm((N_CONV, D_CONV, D_MODEL), D_CONV ** -0.5),
        'cmod_b_pw2': nrm((N_CONV, D_MODEL), 0.02),
        'final_g': 1.0 + nrm((D_MODEL,), 0.05),
    }


def reference(x_prompt, x_sample, c_prompt, c_sample, state_ssd, cache_ssd_conv, cache_cmod_conv,
              mod_w, mod_b, norm_g, ffn_w1, ffn_w3, ffn_w2,
              ssd_w_in, ssd_conv_w, ssd_conv_b, ssd_dt_bias, ssd_a_log, ssd_d, ssd_norm_g, ssd_w_out,
              cmod_w_pw1, cmod_b_pw1, cmod_dw_w, cmod_dw_b, cmod_ln_g, cmod_ln_b, cmod_w_pw2, cmod_b_pw2,
              final_g):
    weights = (mod_w, mod_b, norm_g, ffn_w1, ffn_w3, ffn_w2,
               ssd_w_in, ssd_conv_w, ssd_conv_b, ssd_dt_bias, ssd_a_log, ssd_d, ssd_norm_g, ssd_w_out,
               cmod_w_pw1, cmod_b_pw1, cmod_dw_w, cmod_dw_b, cmod_ln_g, cmod_ln_b, cmod_w_pw2, cmod_b_pw2,
               final_g)
    bp = x_prompt.shape[0]
    zero_state = jnp.zeros((N_SSD, bp) + state_ssd.shape[2:], x_prompt.dtype)
    zero_ssd_conv = jnp.zeros((N_SSD, bp) + cache_ssd_conv.shape[2:], x_prompt.dtype)
    zero_cmod_conv = jnp.zeros((N_CONV, bp) + cache_cmod_conv.shape[2:], x_prompt.dtype)
    y_prompt, st_p, sc_p, cc_p = trunk(x_prompt, c_prompt, zero_state, zero_ssd_conv, zero_cmod_conv, *weights)
    y_sample, st_s, sc_s, cc_s = trunk(x_sample, c_sample, state_ssd, cache_ssd_conv, cache_cmod_conv, *weights)
    return (y_prompt, y_sample, st_p, sc_p, cc_p, st_s, sc_s, cc_s)
```

```python
from contextlib import ExitStack
import numpy as np
import concourse.bass as bass
import concourse.mybir as mybir
from concourse.bass_utils import run_bass_kernel_spmd

F32 = mybir.dt.float32
BF16 = mybir.dt.bfloat16
AF = mybir.ActivationFunctionType
ALU = mybir.AluOpType

D = 1024
DFF = 2816
DIN = 2048
NH = 32
HP = 64
NST = 128
CONVD = 3072
PROJ = 5152
KW = 31
EPS = 1e-6
SEQ = 8192
DEC = 16
NCORES = 8

PE, ACT, DVE, POOL, SP = "pe", "act", "dve", "pool", "sp"
ENGS = [PE, ACT, DVE, POOL, SP]
EPOCH = 30000


def _esz(dt):
    return 2 if dt == BF16 else 4


class Op:
    __slots__ = ("eng", "fn", "deps", "is_dma", "key", "count", "tickno", "needed", "idx", "dma_wait")

    def __init__(self, eng, fn):
        self.eng = eng
        self.fn = fn
        self.deps = []
        self.is_dma = False
        self.key = None
        self.count = 0
        self.tickno = 0
        self.needed = False
        self.dma_wait = {}


class Prog:
    def __init__(self):
        self.q = {e: [] for e in ENGS}
        self.recs = {}
        self.dma_counts = {}
        self.ordered_keys = {f"cast{i}" for i in range(8)}
        self.nops = 0

    @staticmethod
    def region(ap):
        if not hasattr(ap, "tensor"):
            ap = ap[:]
        t = ap.tensor
        name = t.name
        esz = _esz(ap.dtype)
        dims = ap.ap
        off = int(ap.offset)
        kind = type(t).__name__
        if kind.startswith("DRam"):
            ext = sum(abs(s) * (c - 1) for s, c in dims) + 1
            return name, 0, 1, off * esz, (off + ext) * esz, "dram"
        if kind.startswith("PSum"):
            return name, 0, 128, 0, 1 << 30, "psum"
        row = dims[0][0]
        p0 = off // row
        c0 = off % row
        p1 = p0 + dims[0][1]
        ext = sum(abs(s) * (c - 1) for s, c in dims[1:]) + 1
        return name, p0, p1, c0 * esz, (c0 + ext) * esz, "sbuf"

    def _access(self, op, ap, is_write):
        name, p0, p1, b0, b1, kind = self.region(ap)
        lst = self.recs.setdefault(name, [])
        psum = kind == "psum"
        wr = is_write or psum
        keep = []
        hit = False
        for r in lst:
            if r[0] < p1 and p0 < r[1] and r[2] < b1 and b0 < r[3]:
                hit = True
                w = r[4]
                if w is not None and w is not op:
                    op.deps.append((w, "raw" if not is_write else "waw"))
                if wr:
                    for rd in r[5]:
                        if rd is not op:
                            op.deps.append((rd, "war"))
                    x0, x1 = max(r[2], b0), min(r[3], b1)
                    if r[2] < x0:
                        keep.append([r[0], r[1], r[2], x0, r[4], list(r[5])])
                    if x1 < r[3]:
                        keep.append([r[0], r[1], x1, r[3], r[4], list(r[5])])
                    if r[0] < p0:
                        keep.append([r[0], p0, x0, x1, r[4], list(r[5])])
                    if p1 < r[1]:
                        keep.append([p1, r[1], x0, x1, r[4], list(r[5])])
                    continue
                r[5].append(op)
            keep.append(r)
        if wr:
            keep.append([p0, p1, b0, b1, op, []])
        elif not hit:
            keep.append([p0, p1, b0, b1, None, [op]])
        self.recs[name] = keep

    def _finish(self, op, reads, writes):
        for ap in reads:
            self._access(op, ap, False)
        for ap in writes:
            self._access(op, ap, True)
        deps = []
        seen = set()
        for d, kind in op.deps:
            if id(d) in seen:
                continue
            if d.is_dma:
                pass
            elif d.eng == op.eng and not op.is_dma:
                if op.eng == PE:
                    continue
            seen.add(id(d))
            deps.append(d)
            if d.is_dma:
                cnt = d.count if d.key in self.ordered_keys else max(self.dma_counts[d.key], d.count)
                op.dma_wait[d.key] = max(op.dma_wait.get(d.key, 0), cnt)
            else:
                d.needed = True
        op.deps = deps
        op.idx = self.nops
        self.nops += 1
        self.q[op.eng].append(op)
        return op

    def op(self, eng, fn, reads=(), writes=()):
        return self._finish(Op(eng, fn), list(reads), list(writes))

    def dma(self, queue, out, in_, key):
        o = Op(queue, None)
        o.is_dma = True
        o.key = key
        self.dma_counts.setdefault(key, 0)
        o.fn = lambda e, out=out, in_=in_: e.dma_start(out=out, in_=in_)
        self._finish(o, [in_], [out])
        if key in self.ordered_keys and self.dma_counts[key] > 0:
            o.dma_wait[key] = self.dma_counts[key]
        self.dma_counts[key] += 16
        o.count = self.dma_counts[key]
        return o

    def barrier_all_dma(self, queue):
        o = Op(queue, None)
        o.fn = None
        o.dma_wait = dict(self.dma_counts)
        o.idx = self.nops
        self.nops += 1
        self.q[queue].append(o)

    def emit(self, nc, stack):
        engobj = {PE: "tensor", ACT: "scalar", DVE: "vector", POOL: "gpsimd", SP: "sync"}
        nt = {}
        for e in ENGS:
            n = 0
            for o in self.q[e]:
                if o.needed and not o.is_dma:
                    n += 1
                    o.tickno = n
            nt[e] = n
        esems = {}
        for e in ENGS:
            ne = max(1, (nt[e] + EPOCH - 1) // EPOCH)
            esems[e] = [stack.enter_context(nc.semaphore(f"t_{e}_{i}")) for i in range(ne)]
        dsems = {k: stack.enter_context(nc.semaphore(f"d_{k}")) for k in self.dma_counts}
        for k, v in self.dma_counts.items():
            assert v < 32000, (k, v)
        block = stack.enter_context(nc.Block())

        def make(e):
            ops = self.q[e]

            def body(eng):
                waited = {}
                for o in ops:
                    need = {}
                    for d in o.deps:
                        if d.is_dma:
                            continue
                        if need.get(d.eng, 0) < d.tickno:
                            need[d.eng] = d.tickno
                    for de, t in need.items():
                        if waited.get(de, 0) >= t:
                            continue
                        waited[de] = t
                        eng.wait_ge(esems[de][(t - 1) // EPOCH], (t - 1) % EPOCH + 1)
                    for k, v in o.dma_wait.items():
                        if waited.get(("dma", k), 0) >= v:
                            continue
                        waited[("dma", k)] = v
                        eng.wait_ge(dsems[k], v)
                    if o.fn is None:
                        continue
                    ins = o.fn(eng)
                    if o.is_dma:
                        ins.then_inc(dsems[o.key], 16)
                    elif o.tickno:
                        ins.then_inc(esems[e][(o.tickno - 1) // EPOCH], 1)
            return body

        for e in ENGS:
            getattr(block, engobj[e])(make(e))


def _vec_rows(inp):
    items = []

    def add(name, v):
        v = np.asarray(v, np.float32).reshape(-1)
        n = (v.size + 127) // 128 * 128
        if n != v.size:
            v = np.concatenate([v, np.zeros(n - v.size, np.float32)])
        items.append((name, v.reshape(-1, 128)))

    for i in range(2):
        for s in range(3):
            add(f"ng{i}{s}", inp["norm_g"][i, s])
    add("fg", inp["final_g"])
    for i in range(2):
        add(f"mb{i}", inp["mod_b"][i])
    for k in range(4):
        add(f"scw{k}", inp["ssd_conv_w"][0, k])
    add("scb", inp["ssd_conv_b"][0])
    add("sng", inp["ssd_norm_g"][0])
    add("dexp", np.repeat(inp["ssd_d"][0], HP))
    add("dtb", inp["ssd_dt_bias"][0])
    add("alog", inp["ssd_a_log"][0])
    add("bp1", inp["cmod_b_pw1"][0])
    for k in range(KW):
        add(f"dw{k}", inp["cmod_dw_w"][0, k])
    add("dwb", inp["cmod_dw_b"][0])
    add("lng", inp["cmod_ln_g"][0])
    add("lnb", inp["cmod_ln_b"][0])
    add("bp2", inp["cmod_b_pw2"][0])
    rows = {}
    r = 0
    mats = []
    for name, m in items:
        rows[name] = r
        r += m.shape[0]
        mats.append(m)
    tot = (r + 127) // 128 * 128
    mats.append(np.zeros((tot - r, 128), np.float32))
    return np.ascontiguousarray(np.concatenate(mats, 0)), rows


_VROWS = None


def _vrows_static():
    global _VROWS
    if _VROWS is None:
        fake = {
            "norm_g": np.zeros((2, 3, D)), "final_g": np.zeros(D), "mod_b": np.zeros((2, 9 * D)),
            "ssd_conv_w": np.zeros((1, 4, CONVD)), "ssd_conv_b": np.zeros((1, CONVD)),
            "ssd_norm_g": np.zeros((1, DIN)), "ssd_d": np.zeros((1, NH)), "ssd_dt_bias": np.zeros((1, NH)),
            "ssd_a_log": np.zeros((1, NH)), "cmod_b_pw1": np.zeros((1, 2 * D)),
            "cmod_dw_w": np.zeros((1, KW, D)), "cmod_dw_b": np.zeros((1, D)), "cmod_ln_g": np.zeros((1, D)),
            "cmod_ln_b": np.zeros((1, D)), "cmod_b_pw2": np.zeros((1, D)),
        }
        v, rows = _vec_rows(fake)
        _VROWS = (v.shape[0], rows)
    return _VROWS


def _consts():
    c = np.zeros((128, 640), np.float32)
    c[:, 0:128] = np.eye(128)
    c[:, 128:256] = 1.0
    p = np.arange(128)[:, None]
    l = np.arange(64)[None, :]
    c[:, 256:320] = ((p % 64) <= l)
    q = np.arange(128)[None, :]
    c[:, 320:448] = ((p // 64) == (q // 64)) & (p > q)
    c[:, 448:576] = ((p // 64) == (q // 64)) & (p <= q)
    c[:, 576:640] = ((p % 64) == l)
    return c


NSLOT = 4
SLOTW = 4096
ARENA_WORDS = 30500


class Arena:
    def __init__(self, t):
        self.t = t
        self.off = 0
        self.peak = 0

    def mark(self):
        return self.off

    def release(self, m):
        self.off = m

    def alloc(self, shape, dt):
        n = int(np.prod(shape))
        words = n if dt == F32 else (n + 1) // 2
        words = (words + 7) // 8 * 8
        assert self.off + words <= ARENA_WORDS, ("arena overflow", self.off, words)
        v = self.t[:, self.off:self.off + words]
        self.off += words
        self.peak = max(self.peak, self.off)
        if dt == BF16:
            v = v.bitcast(BF16)
        v = v[:, :n]
        if len(shape) == 2:
            v = v.rearrange("p (a b) -> p a b", a=shape[0])
        elif len(shape) == 3:
            v = v.rearrange("p (a b c) -> p a b c", a=shape[0], b=shape[1])
        return v


def bc(ap, shape):
    return ap.to_broadcast(list(shape))


class Builder:
    def __init__(self, S, dbg=False):
        self.S = S
        self.nc = nc = bass.Bass("TRN2", target_bir_lowering=False)
        self.p = Prog()
        self.stack = ExitStack()
        self.dbg = dbg
        nvr, self.vr = _vrows_static()
        self.nvr = nvr

        def din(name, shape):
            return nc.dram_tensor(name, list(shape), F32, kind="ExternalInput").ap()

        def dout(name, shape):
            return nc.dram_tensor(name, list(shape), F32, kind="ExternalOutput").ap()

        self.xp = din("xp", [S, D])
        self.xs = din("xs", [DEC, D])
        self.c2 = din("c2", [2, D])
        self.st_in = din("st_in", [NH * HP, NST])
        self.sc_in = din("sc_in", [3, CONVD])
        self.cc_in = din("cc_in", [KW - 1, D])
        self.consts = din("consts", [128, 640])
        self.vecs = din("vecs", [nvr, 128])
        self.mod_w = din("mod_w", [2, D, 9 * D])
        self.w = {
            "w1": din("ffn_w1", [2, 2, D, DFF]), "w3": din("ffn_w3", [2, 2, D, DFF]),
            "w2": din("ffn_w2", [2, 2, DFF, D]), "win": din("ssd_w_in", [1, D, PROJ]),
            "wout": din("ssd_w_out", [1, DIN, D]), "pw1": din("cmod_w_pw1", [1, D, 2 * D]),
            "pw2": din("cmod_w_pw2", [1, D, D]),
        }
        self.yp = dout("yp", [S, D])
        self.ys = dout("ys", [DEC, D])
        self.o_st = [dout("st_p", [NH * HP, NST]), dout("st_s", [NH * HP, NST])]
        self.o_sc = [dout("sc_p", [3, CONVD]), dout("sc_s", [3, CONVD])]
        self.o_cc = [dout("cc_p", [KW - 1, D]), dout("cc_s", [KW - 1, D])]
        if dbg:
            self.dbgo = dout("dbg", [8, 128, 8, 512])

        st = self.stack

        def sb(name, shape, dt=F32):
            return st.enter_context(nc.sbuf_tensor(name, list(shape), dt))

        self.cst = sb("cst", [128, 640])
        self.bdb = sb("bdb", [128, 128], BF16)
        self.mskb = sb("mskb", [128, 128], BF16)
        self.ident = self.cst[:, 0:128]
        self.ones = self.cst[:, 128:256]
        self.tri = self.cst[:, 256:320]
        self.bd = self.cst[:, 320:448]
        self.bdi = self.cst[:, 448:576]
        self.idl = self.cst[:, 576:640]
        self.vcol = sb("vcol", [128, nvr])
        self.modc = sb("modc", [128, 2, 72, 2])
        self.der = sb("der", [128, 2, 2, 6, 8])
        self.acol = sb("acol", [128, 1])
        self.xres = sb("xres", [128, 8, 512])
        self.H = sb("H", [128, NH, HP])
        self.Hb = sb("Hb", [128, 16, 2, 128], BF16)
        self.xl32 = sb("xl32", [128, 24, 3])
        self.uhist = sb("uhist", [128, 8, KW - 1])
        self.stg_in = [sb(f"stgi{i}", [128, D]) for i in range(2)]
        self.ring = [sb(f"ring{i}", [128, SLOTW], BF16) for i in range(NSLOT)]
        self.arena_t = sb("arena", [128, ARENA_WORDS])
        self.A = Arena(self.arena_t)
        self.ps = [st.enter_context(nc.psum_tensor(f"ps{i}", [128, 512], F32)) for i in range(8)]
        self.psi = 0
        self.sbank = self.ps[7]
        self.mbank = self.ps[6]
        self.sqt = [sb(f"sqt{i}", [128, 512]) for i in range(2)]
        self._stat_pending = []
        self.vsum = sb("vsum", [128, 512])
        self.prefetched = set()
        self.wplan = {}
        self.scr = {}
        for name, K, M, MW, cnt in [("w1", D, DFF, 512, 4), ("w3", D, DFF, 512, 4), ("w2", DFF, D, 128, 4),
                                    ("win", D, PROJ, 512, 1), ("wout", DIN, D, 256, 1),
                                    ("pw1", D, 2 * D, 512, 1), ("pw2", D, D, 512, 1)]:
            ns = (M + MW - 1) // MW
            self.wplan[name] = (K, M, MW, ns)
            self.scr[name] = nc.dram_tensor("scr_" + name, [cnt, ns, 128, SLOTW], BF16).ap()
        self.wplan["dgs"] = (24 * 128, 128 * 4, 128, 4)
        self.wplan["dgc"] = (KW * 128, 128 * 8, 128, 8)
        self.scr["dgs"] = nc.dram_tensor("scr_dgs", [1, 4, 128, SLOTW], BF16).ap()
        self.scr["dgc"] = nc.dram_tensor("scr_dgc", [1, 8, 128, SLOTW], BF16).ap()
        self.build_wseq()
        self.wpos = 0
        self.wloaded = 0
        self.G0 = None
        self.depth = NSLOT
        self.xslots = []

    def bank(self):
        b = self.ps[self.psi]
        self.psi = (self.psi + 1) % 6
        return b

    def vc(self, name, n=1, off=0):
        r = self.vr[name] + off
        return self.vcol[:, r:r + n]

    def build_wseq(self):
        seq = []

        def slabs(name, idx, order=None):
            K, M, MW, ns = self.wplan[name]
            for s in (order if order is not None else range(ns)):
                mw = min(MW, M - s * MW)
                seq.append((name, idx, s, K // 128, mw))

        def ffn(i, w):
            K, M, MW, ns = self.wplan["w1"]
            for s in range(ns):
                mw = min(MW, M - s * MW)
                seq.append(("w1", i * 2 + w, s, 8, mw))
                seq.append(("w3", i * 2 + w, s, 8, mw))
            slabs("w2", i * 2 + w)

        ffn(0, 0)
        slabs("win", 0, order=[4, 5, 6, 7, 8, 9, 10])
        slabs("dgs", 0)
        slabs("win", 0, order=[0, 1, 2, 3])
        slabs("wout", 0)
        ffn(0, 1)
        ffn(1, 0)
        slabs("pw1", 0)
        slabs("dgc", 0)
        slabs("pw2", 0)
        ffn(1, 1)
        self.wseq = seq

    def cast_weights(self):
        done = set()
        self._ncast = 0
        for (name, idx, s, KC, mw) in self.wseq:
            if (name, idx, s) in done or name in ("dgs", "dgc"):
                continue
            done.add((name, idx, s))
            K, M, MW, ns = self.wplan[name]
            wap = self.w[name]
            wap = wap[idx // 2, idx % 2] if name in ("w1", "w3", "w2") else wap[0]
            src = wap.rearrange("(k p) m -> p k m", p=128)[:, :, s * MW:s * MW + mw]
            dst = self.scr[name][idx, s][:, :KC * mw].rearrange("p (k m) -> p k m", k=KC)
            o = self.p.dma(POOL, dst, src, key=f"cast{self._ncast % 8}")
            if self._ncast == 8:
                for k_ in ("mw0", "mw1"):
                    o.dma_wait[k_] = max(o.dma_wait.get(k_, 0), self.p.dma_counts.get(k_, 0) - 32)
            self._ncast += 1

    def _slot_of(self, g):
        if self.G0 is None or g < self.G0 + NSLOT:
            return self.ring[g % NSLOT], f"w{g % NSLOT}"
        i = (g - self.G0 - NSLOT) % len(self.xslots)
        return self.xslots[i], f"wx{i}"

    def _load_slab(self, g):
        ntile_seq = len(self.wseq)
        if g >= self.total_slabs:
            return
        name, idx, s, KC, mw = self.wseq[g % ntile_seq]
        slot, key = self._slot_of(g)
        src = self.scr[name][idx, s][:, :KC * mw]
        dst = slot[:, :KC * mw]
        self.p.dma(SP, dst, src, key=key)

    def start_wstream(self, ntiles):
        self.total_slabs = ntiles * len(self.wseq)
        for g in range(NSLOT):
            self._load_slab(g)
        self.wloaded = NSLOT

    def enter_sample_mode(self, nextra=8):
        self.xslots = [self.A.alloc([SLOTW], BF16) for _ in range(nextra)]
        self.G0 = self.wpos
        self.depth = NSLOT + nextra
        for g in range(self.G0 + NSLOT, self.G0 + NSLOT + nextra):
            self._load_slab(g)

    def next_slab(self, expect):
        g = self.wpos
        name, idx, s, KC, mw = self.wseq[g % len(self.wseq)]
        assert name == expect, (name, expect)
        self.wpos += 1
        slot, _ = self._slot_of(g)
        v = slot[:, :KC * mw].rearrange("p (k m) -> p k m", k=KC)
        return v, mw, g

    def slab_done(self, g):
        if self.G0 is None:
            self._load_slab(g + NSLOT)
        elif g >= self.G0 + NSLOT:
            self._load_slab(g + len(self.xslots))

    def setup(self):
        p, A = self.p, self.A
        ident = self.ident
        p.dma(SP, self.cst[:], self.consts, key="c0")
        p.op(DVE, lambda e: e.tensor_copy(out=self.bdb[:], in_=self.bd), reads=[self.bd], writes=[self.bdb[:]])
        p.op(DVE, lambda e: e.tensor_copy(out=self.mskb[:, 0:64], in_=self.tri), reads=[self.tri], writes=[self.mskb[:, 0:64]])
        p.op(DVE, lambda e: e.tensor_copy(out=self.mskb[:, 64:128], in_=self.idl), reads=[self.idl], writes=[self.mskb[:, 64:128]])
        m0 = A.mark()
        nblk = self.nvr // 128
        stg = [A.alloc([128], F32) for _ in range(2)]
        for b in range(nblk):
            s = stg[b % 2]
            p.dma(SP, s, self.vecs[b * 128:(b + 1) * 128, :], key=f"vs{b % 2}")
            bk = self.bank()
            p.op(PE, lambda e, bk=bk, s=s: e.transpose(bk[:, 0:128], s, ident), reads=[s, ident], writes=[bk])
            dst = self.vcol[:, b * 128:(b + 1) * 128]
            p.op(DVE, lambda e, bk=bk, dst=dst: e.tensor_copy(out=dst, in_=bk[:, 0:128]), reads=[bk], writes=[dst])
        al = self.vc("alog")
        p.op(ACT, lambda e: e.activation(out=self.acol[0:32, :], in_=al[0:32, :], func=AF.Exp),
             reads=[al[0:32, :]], writes=[self.acol[0:32, :]])
        p.op(DVE, lambda e: e.tensor_scalar(out=self.acol[0:32, :], in0=self.acol[0:32, :], scalar1=-1.0, scalar2=None,
                                            op0=ALU.mult),
             reads=[self.acol[0:32, :]], writes=[self.acol[0:32, :]])
        c2sb = A.alloc([D], F32)
        p.dma(SP, c2sb[0:2, :], self.c2, key="c1")
        cact = A.alloc([8, 2], F32)
        bk = self.bank()
        for k in range(8):
            p.op(PE, lambda e, k=k, bk=bk: e.transpose(bk[:, 2 * k:2 * k + 2], c2sb[0:2, k * 128:(k + 1) * 128], ident[0:2, 0:2]),
                 reads=[c2sb[0:2, k * 128:(k + 1) * 128], ident[0:2, 0:2]], writes=[bk])
        p.op(ACT, lambda e, bk=bk: e.activation(out=cact, in_=bk[:, 0:16].rearrange("p (k g) -> p k g", k=8), func=AF.Silu),
             reads=[bk], writes=[cact])
        wst = [A.alloc([8, 512], F32) for _ in range(2)]
        wsb = [A.alloc([8, 512], BF16) for _ in range(2)]
        rowb = [A.alloc([512], F32) for _ in range(2)]
        cactb = A.alloc([8, 2], BF16)
        p.op(DVE, lambda e: e.tensor_copy(out=cactb, in_=cact), reads=[cact], writes=[cactb])
        n = 0
        for i in range(2):
            bkT = self.sbank
            for s in range(18):
                wt = wst[n % 2]
                wb = wsb[n % 2]
                rb = rowb[n % 2]
                p.dma(SP, wt, self.mod_w[i].rearrange("(k p) m -> p k m", p=128)[:, :, s * 512:(s + 1) * 512], key=f"mw{n % 2}")
                if n % 2 == 0:
                    p.op(DVE, lambda e, wt=wt, wb=wb: e.tensor_copy(out=wb, in_=wt), reads=[wt], writes=[wb])
                else:
                    p.op(ACT, lambda e, wt=wt, wb=wb: e.activation(out=wb, in_=wt, func=AF.Copy), reads=[wt], writes=[wb])
                n += 1
                bk = self.bank()
                for k in range(8):
                    p.op(PE, lambda e, bk=bk, wb=wb, k=k: e.matmul(bk[0:2, :], cactb[:, k, :], wb[:, k, :], start=(k == 0), stop=(k == 7)),
                         reads=[cactb[:, k, :], wb[:, k, :]], writes=[bk])
                p.op(DVE, lambda e, bk=bk, rb=rb: e.tensor_copy(out=rb[0:2, :], in_=bk[0:2, :]), reads=[bk], writes=[rb[0:2, :]])
                for jj in range(4):
                    j = 4 * s + jj
                    p.op(PE, lambda e, bkT=bkT, rb=rb, jj=jj, j=j: e.transpose(bkT[:, 2 * j:2 * j + 2], rb[0:2, jj * 128:(jj + 1) * 128], ident[0:2, 0:2]),
                         reads=[rb[0:2, jj * 128:(jj + 1) * 128], ident[0:2, 0:2]], writes=[bkT])
            mb = self.vc(f"mb{i}", 72)
            dst = self.modc[:, i]
            p.op(DVE, lambda e, bkT=bkT, dst=dst, mb=mb: e.tensor_tensor(
                out=dst, in0=bkT[:, 0:144].rearrange("p (j g) -> p j g", j=72),
                in1=bc(mb.unsqueeze(2), [128, 72, 2]), op=ALU.add), reads=[bkT, mb], writes=[dst])
        for g in range(2):
            for i in range(2):
                for s in range(3):
                    sc = self.modc[:, i, (3 * s + 1) * 8:(3 * s + 2) * 8, g]
                    ng = self.vc(f"ng{i}{s}", 8)
                    dst = self.der[:, g, i, s, :]
                    p.op(DVE, lambda e, sc=sc, ng=ng, dst=dst: e.scalar_tensor_tensor(
                        out=dst, in0=sc, scalar=1.0, in1=ng, op0=ALU.add, op1=ALU.mult), reads=[sc, ng], writes=[dst])
                for s, slot in ((0, 3), (2, 4)):
                    gt = self.modc[:, i, (3 * s + 2) * 8:(3 * s + 3) * 8, g]
                    dst = self.der[:, g, i, slot, :]
                    p.op(DVE, lambda e, gt=gt, dst=dst: e.tensor_scalar(out=dst, in0=gt, scalar1=0.5, scalar2=None, op0=ALU.mult),
                         reads=[gt], writes=[dst])
                gt2 = self.modc[:, i, 40:48, g]
                dst = self.der[:, g, i, 5, :]
                b2 = self.vc("bp2", 8)
                p.op(DVE, lambda e, gt2=gt2, dst=dst, b2=b2: e.tensor_tensor(out=dst, in0=gt2, in1=b2, op=ALU.mult),
                     reads=[gt2, b2], writes=[dst])
        dst_ = [A.alloc([SLOTW], BF16) for _ in range(2)]
        nb_ = 0
        r0 = self.vr["scw0"]
        for sl in range(4):
            d = dst_[nb_ % 2][:, 0:24 * 128].rearrange("p (j k q) -> p j k q", j=6, k=4)
            wbase = self.vcol[:, r0 + 6 * sl:r0 + 6 * sl + 1]
            wk = bass.AP(wbase.tensor, wbase.offset, [list(wbase.ap[0]), [1, 6], [24, 4], [0, 128]])
            idb = bass.AP(ident.tensor, ident.offset, [list(ident.ap[0]), [0, 6], [0, 4], [1, 128]])
            p.op(DVE, lambda e, d=d, wk=wk, idb=idb: e.tensor_tensor(out=d, in0=idb, in1=wk, op=ALU.mult),
                 reads=[ident, self.vcol[:, r0:r0 + 96]], writes=[dst_[nb_ % 2][:, 0:24 * 128]])
            p.dma(SP, self.scr["dgs"][0, sl][:, 0:24 * 128], dst_[nb_ % 2][:, 0:24 * 128], key=f"dgb{nb_ % 2}")
            nb_ += 1
        r0 = self.vr["dw0"]
        for c in range(8):
            d = dst_[nb_ % 2][:, 0:KW * 128].rearrange("p (k q) -> p k q", k=KW)
            wbase = self.vcol[:, r0 + c:r0 + c + 1]
            wk = bass.AP(wbase.tensor, wbase.offset, [list(wbase.ap[0]), [8, KW], [0, 128]])
            p.op(DVE, lambda e, d=d, wk=wk: e.tensor_tensor(out=d, in0=bc(ident.unsqueeze(1), [128, KW, 128]), in1=wk, op=ALU.mult),
                 reads=[ident, self.vcol[:, r0:r0 + 8 * KW]], writes=[dst_[nb_ % 2][:, 0:KW * 128]])
            p.dma(SP, self.scr["dgc"][0, c][:, 0:KW * 128], dst_[nb_ % 2][:, 0:KW * 128], key=f"dgb{nb_ % 2}")
            nb_ += 1
        A.release(m0)

    def gsc(self, g, i, s):
        return self.der[:, g, i, s, :]

    def shc(self, g, i, s):
        return self.modc[:, i, (3 * s) * 8:(3 * s + 1) * 8, g]

    def init_state_zero(self):
        p = self.p
        for t in (self.H, self.xl32, self.uhist):
            p.op(DVE, lambda e, t=t: e.memset(t[:], 0.0), writes=[t[:]])
        p.op(DVE, lambda e: e.memset(self.Hb[:], 0.0), writes=[self.Hb[:]])

    def hb_data(self, rows=slice(0, 128)):
        t = self.Hb
        a = t[rows, :, :, 0:64]
        dims = a.ap
        row = dims[0]
        return bass.AP(a.tensor, a.offset, [list(row), [256, 16], [192, 2], [1, 64]])

    def refresh_Hb(self, gq=None):
        p = self.p
        if gq is None:
            dst = self.hb_data()
            src = self.H[:].rearrange("p (c e) q -> p c e q", e=2)
            p.op(ACT, lambda e: e.activation(out=dst, in_=src, func=AF.Copy), reads=[self.H[:]], writes=[self.Hb[:]])
            return
        a = self.Hb[:, 4 * gq:4 * gq + 4, :, 0:64]
        dst = bass.AP(a.tensor, a.offset, [list(a.ap[0]), [256, 4], [192, 2], [1, 64]])
        hsrc = self.H[:, 8 * gq:8 * gq + 8, :]
        src = hsrc.rearrange("p (c e) q -> p c e q", e=2)
        p.op(ACT, lambda e: e.activation(out=dst, in_=src, func=AF.Copy), reads=[hsrc], writes=[self.Hb[:, 4 * gq:4 * gq + 4]])

    def load_sample_state(self):
        p, A, ident = self.p, self.A, self.ident
        m0 = A.mark()
        stg = [A.alloc([128], F32) for _ in range(2)]
        for b in range(16):
            s = stg[b % 2]
            p.dma(SP, s, self.st_in[b * 128:(b + 1) * 128, :], key=f"vs{b % 2}")
            bk = self.bank()
            p.op(PE, lambda e, bk=bk, s=s: e.transpose(bk[:, 0:128], s, ident), reads=[s, ident], writes=[bk])
            dst = self.H[:, 2 * b:2 * b + 2, :]
            p.op(DVE, lambda e, bk=bk, dst=dst: e.tensor_copy(out=dst, in_=bk[:, 0:128].rearrange("p (h q) -> p h q", h=2)),
                 reads=[bk], writes=[dst])
        self.refresh_Hb()
        cs = A.alloc([CONVD], F32)
        p.dma(SP, cs[0:3, :], self.sc_in, key="c2")
        bk = self.bank()
        for j in range(24):
            p.op(PE, lambda e, bk=bk, j=j: e.transpose(bk[:, 3 * j:3 * j + 3], cs[0:3, j * 128:(j + 1) * 128], ident[0:3, 0:3]),
                 reads=[cs[0:3, j * 128:(j + 1) * 128], ident[0:3, 0:3]], writes=[bk])
        p.op(DVE, lambda e, bk=bk: e.tensor_copy(out=self.xl32[:], in_=bk[:, 0:72].rearrange("p (j t) -> p j t", j=24)),
             reads=[bk], writes=[self.xl32[:]])
        cc = A.alloc([D], F32)
        p.dma(SP, cc[0:KW - 1, :], self.cc_in, key="c3")
        bk = self.bank()
        for c in range(8):
            p.op(PE, lambda e, bk=bk, c=c: e.transpose(bk[:, 30 * c:30 * c + 30], cc[0:30, c * 128:(c + 1) * 128], ident[0:30, 0:30]),
                 reads=[cc[0:30, c * 128:(c + 1) * 128], ident[0:30, 0:30]], writes=[bk])
        p.op(DVE, lambda e, bk=bk: e.tensor_copy(out=self.uhist[:], in_=bk[:, 0:240].rearrange("p (c t) -> p c t", c=8)),
             reads=[bk], writes=[self.uhist[:]])
        A.release(m0)

    def store_states(self, g):
        p, A, ident = self.p, self.A, self.ident
        m0 = A.mark()
        stg = [A.alloc([128], F32) for _ in range(2)]
        for b in range(16):
            s = stg[b % 2]
            bk = self.bank()
            src = self.H[:, 2 * b:2 * b + 2, :]
            p.op(PE, lambda e, bk=bk, src=src: e.transpose(bk[:, 0:128], src.rearrange("p h q -> p (h q)"), ident),
                 reads=[src, ident], writes=[bk])
            p.op(DVE, lambda e, bk=bk, s=s: e.tensor_copy(out=s, in_=bk[:, 0:128]), reads=[bk], writes=[s])
            p.dma(POOL, self.o_st[g][b * 128:(b + 1) * 128, :], s, key=f"so{b % 2}")
        cs = A.alloc([CONVD], F32)
        for q in range(6):
            bk = self.bank()
            for jj in range(4):
                j = 4 * q + jj
                src = self.xl32[:, j, :]
                p.op(PE, lambda e, bk=bk, jj=jj, src=src: e.transpose(bk[0:3, jj * 128:(jj + 1) * 128], src, ident),
                     reads=[src, ident], writes=[bk])
            p.op(DVE, lambda e, bk=bk, q=q: e.tensor_copy(out=cs[0:3, q * 512:(q + 1) * 512], in_=bk[0:3, :]),
                 reads=[bk], writes=[cs[0:3, q * 512:(q + 1) * 512]])
        p.dma(POOL, self.o_sc[g], cs[0:3, :], key="so2")
        cc = A.alloc([D], F32)
        for q in range(2):
            bk = self.bank()
            for cq in range(4):
                c = 4 * q + cq
                src = self.uhist[:, c, :]
                p.op(PE, lambda e, bk=bk, cq=cq, src=src: e.transpose(bk[0:30, cq * 128:(cq + 1) * 128], src, ident),
                     reads=[src, ident], writes=[bk])
            p.op(DVE, lambda e, bk=bk, q=q: e.tensor_copy(out=cc[0:30, q * 512:(q + 1) * 512], in_=bk[0:30, :]),
                 reads=[bk], writes=[cc[0:30, q * 512:(q + 1) * 512]])
        p.dma(POOL, self.o_cc[g], cc[0:30, :], key="so3")
        self._pending_store_bufs = [stg[0], stg[1], cs[0:3, :], cc[0:30, :]]
        A.release(m0)

    def prefetch_x(self, src, r0, TT):
        nb = (TT + 127) // 128
        for b in range(min(2, nb)):
            rows = min(128, TT - 128 * b)
            self.p.dma(ACT, self.stg_in[b % 2][0:rows, :], src[r0 + 128 * b:r0 + 128 * b + rows, :], key=f"xi{b % 2}")
            self.prefetched.add((src.tensor.name, r0, b))

    def load_x(self, src, r0, TT):
        p, ident = self.p, self.ident
        nb = (TT + 127) // 128
        for b in range(nb):
            rows = min(128, TT - 128 * b)
            s = self.stg_in[b % 2]
            if (src.tensor.name, r0, b) not in self.prefetched:
                p.dma(ACT, s[0:rows, :], src[r0 + 128 * b:r0 + 128 * b + rows, :], key=f"xi{b % 2}")
            for half in range(2):
                bk = self.bank()
                for i in range(4):
                    c = 4 * half + i
                    p.op(PE, lambda e, bk=bk, i=i, c=c, s=s, rows=rows: e.transpose(
                        bk[:, i * rows:(i + 1) * rows], s[0:rows, c * 128:(c + 1) * 128], ident[0:rows, 0:rows]),
                        reads=[s[0:rows, c * 128:(c + 1) * 128], ident[0:rows, 0:rows]], writes=[bk])
                dst = self.xres[:, 4 * half:4 * half + 4, 128 * b:128 * b + rows]
                p.op(ACT, lambda e, bk=bk, dst=dst, rows=rows: e.activation(
                    out=dst, in_=bk[:, 0:4 * rows].rearrange("p (c t) -> p c t", c=4), func=AF.Copy),
                    reads=[bk], writes=[dst])
        for m in range(8):
            self.stat_push(TT, m)

    def store_y(self, yT, dst, r0, TT):
        p, ident, A = self.p, self.ident, self.A
        nb = (TT + 127) // 128
        so = [A.alloc([D], F32) for _ in range(2)]
        for b in range(nb):
            rows = min(128, TT - 128 * b)
            s = so[b % 2]
            for half in range(2):
                bk = self.bank()
                for i in range(4):
                    c = 4 * half + i
                    src = yT[:, c, 128 * b:128 * b + rows]
                    p.op(PE, lambda e, bk=bk, i=i, src=src, rows=rows: e.transpose(
                        bk[0:rows, i * 128:(i + 1) * 128], src, ident), reads=[src, ident], writes=[bk])
                d2 = s[0:rows, half * 512:(half + 1) * 512]
                p.op(ACT, lambda e, bk=bk, d2=d2, rows=rows: e.activation(out=d2, in_=bk[0:rows, :], func=AF.Copy),
                     reads=[bk], writes=[d2])
            p.dma(ACT, dst[r0 + 128 * b:r0 + 128 * b + rows, :], s[0:rows, :], key=f"yo{b % 2}")

    def acc_new(self, bank, nch, TT, square):
        return {"bank": bank, "nch": nch, "TT": TT, "square": square, "pending": []}

    def acc_push(self, acc, chunk, m):
        p, TT, nch, bank = self.p, acc["TT"], acc["nch"], acc["bank"]
        ssum = (self.sqt[1] if acc["square"] else self.vsum)[:, 0:TT]
        if acc["square"]:
            sq = self.sqt[0][:, 0:TT]
            p.op(ACT, lambda e: e.activation(out=sq, in_=chunk, func=AF.Square), reads=[chunk], writes=[sq])
            src = sq
        else:
            src = chunk
        if m == 0:
            p.op(DVE, lambda e: e.tensor_copy(out=ssum, in_=src), reads=[src], writes=[ssum])
        else:
            p.op(DVE, lambda e: e.tensor_tensor(out=ssum, in0=ssum, in1=src, op=ALU.add), reads=[ssum, src], writes=[ssum])
        if m == nch - 1:
            p.op(PE, lambda e: e.matmul(bank[:, 0:TT], self.ones, ssum, start=True, stop=True), reads=[self.ones, ssum], writes=[bank])

    def stat_push(self, TT, m, nch=8):
        if m == 0:
            self._racc = self.acc_new(self.sbank, nch, TT, True)
        self.acc_push(self._racc, self.xres[:, m, 0:TT], m)

    def rms(self, src, nch, TT, scale_cols, bias_cols, out, func=None, pre_stats=False, mid=None):
        p, A = self.p, self.A
        m0 = A.mark()
        tmp = A.alloc([nch, TT], F32)
        rstd = A.alloc([TT], F32)
        hh = max(1, nch // 2)
        if pre_stats:
            bk = self.sbank
        else:
            for h0 in range(0, nch, hh):
                p.op(ACT, lambda e, h0=h0: e.activation(out=tmp[:, h0:h0 + hh, :], in_=src[:, h0:h0 + hh, :], func=AF.Square),
                     reads=[src[:, h0:h0 + hh, :]], writes=[tmp[:, h0:h0 + hh, :]])
            bk = self.bank()
            for c in range(nch):
                p.op(PE, lambda e, bk=bk, c=c: e.matmul(bk[:, 0:TT], self.ones, tmp[:, c, :], start=(c == 0), stop=(c == nch - 1)),
                     reads=[self.ones, tmp[:, c, :]], writes=[bk])
        p.op(ACT, lambda e, bk=bk: e.activation(out=rstd, in_=bk[:, 0:TT], func=AF.Sqrt, bias=EPS, scale=1.0 / (128 * nch)),
             reads=[bk], writes=[rstd])
        p.op(DVE, lambda e: e.reciprocal(out=rstd, in_=rstd), reads=[rstd], writes=[rstd])
        q4 = max(1, nch // 4)
        for h0 in range(0, nch, q4):
            p.op(DVE, lambda e, h0=h0: e.tensor_tensor(out=tmp[:, h0:h0 + q4, :], in0=src[:, h0:h0 + q4, :],
                                                       in1=bc(rstd.unsqueeze(1), [128, q4, TT]), op=ALU.mult),
                 reads=[src[:, h0:h0 + q4, :], rstd], writes=[tmp[:, h0:h0 + q4, :]])
        if mid is not None:
            mid()
        for c in range(nch):
            sc = scale_cols[:, c:c + 1]
            bi = bias_cols[:, c:c + 1] if bias_cols is not None else 0.0
            rd = [tmp[:, c, :], sc] + ([bi] if bias_cols is not None else [])
            p.op(ACT, lambda e, c=c, sc=sc, bi=bi: e.activation(out=out[:, c, :], in_=tmp[:, c, :], func=(func or AF.Identity),
                                                               bias=bi, scale=sc), reads=rd, writes=[out[:, c, :]])
        A.release(m0)

    def ffn(self, TT, g, i, s, w):
        p, A = self.p, self.A
        xres = self.xres[:, :, 0:TT]
        m0 = A.mark()
        hbf = A.alloc([8, TT], BF16)
        hid = A.alloc([22, TT], BF16)
        sil = [A.alloc([TT], F32) for _ in range(2)]
        self.rms(xres, 8, TT, self.gsc(g, i, s), self.shc(g, i, s), hbf, pre_stats=True)
        gate = self.der[:, g, i, 3 if s == 0 else 4, :]
        j = 0
        for sl in range(6):
            w1s, mw, g1 = self.next_slab("w1")
            w3s, _, g3 = self.next_slab("w3")
            for jj in range(mw // 128):
                pa = self.bank()
                for k in range(8):
                    p.op(PE, lambda e, pa=pa, k=k, jj=jj, w1s=w1s: e.matmul(
                        pa[:, 0:TT], w1s[:, k, jj * 128:(jj + 1) * 128], hbf[:, k, :], start=(k == 0), stop=(k == 7)),
                        reads=[w1s[:, k, jj * 128:(jj + 1) * 128], hbf[:, k, :]], writes=[pa])
                pb = self.bank()
                for k in range(8):
                    p.op(PE, lambda e, pb=pb, k=k, jj=jj, w3s=w3s: e.matmul(
                        pb[:, 0:TT], w3s[:, k, jj * 128:(jj + 1) * 128], hbf[:, k, :], start=(k == 0), stop=(k == 7)),
                        reads=[w3s[:, k, jj * 128:(jj + 1) * 128], hbf[:, k, :]], writes=[pb])
                st = sil[j % 2]
                p.op(ACT, lambda e, pa=pa, st=st: e.activation(out=st, in_=pa[:, 0:TT], func=AF.Silu), reads=[pa], writes=[st])
                p.op(DVE, lambda e, pb=pb, st=st, j=j: e.tensor_tensor(out=hid[:, j, :], in0=st, in1=pb[:, 0:TT], op=ALU.mult),
                     reads=[st, pb], writes=[hid[:, j, :]])
                j += 1
            self.slab_done(g1)
            self.slab_done(g3)
        for m in range(8):
            w2s, _, g2 = self.next_slab("w2")
            pc = self.bank()
            for k in range(22):
                p.op(PE, lambda e, pc=pc, k=k, w2s=w2s: e.matmul(pc[:, 0:TT], w2s[:, k, :], hid[:, k, :], start=(k == 0), stop=(k == 21)),
                     reads=[w2s[:, k, :], hid[:, k, :]], writes=[pc])
            self.slab_done(g2)
            xm = xres[:, m, :]
            p.op(DVE, lambda e, pc=pc, xm=xm, m=m: e.scalar_tensor_tensor(
                out=xm, in0=pc[:, 0:TT], scalar=gate[:, m:m + 1], in1=xm, op0=ALU.mult, op1=ALU.add),
                reads=[pc, gate[:, m:m + 1], xm], writes=[xm])
            self.stat_push(TT, m)
        A.release(m0)

    def ssd(self, TT, g):
        p, A, ident, ones, tri, bd = self.p, self.A, self.ident, self.ones, self.tri, self.bd
        L = min(64, TT)
        GT = min(128, TT)
        nchk = GT // L
        ngrp = TT // GT
        R = GT
        xres = self.xres[:, :, 0:TT]
        H, Hb = self.H, self.Hb
        m0 = A.mark()
        hbf = A.alloc([8, TT], BF16)
        self.rms(xres, 8, TT, self.gsc(g, 0, 1), self.shc(g, 0, 1), hbf, pre_stats=True)
        xs32 = A.alloc([16, TT], F32)
        BT = A.alloc([4, TT], F32)
        CT = A.alloc([4, TT], F32)
        dtT = A.alloc([TT], F32)
        dtAT = A.alloc([TT], F32)
        m1 = A.mark()
        xbc = A.alloc([24, TT + 3], BF16)
        tv = [A.alloc([TT], F32) for _ in range(3)]
        p.op(DVE, lambda e: e.tensor_copy(out=xbc[:, :, 0:3], in_=self.xl32[:]), reads=[self.xl32[:]], writes=[xbc[:, :, 0:3]])
        for sl in range(4, 10):
            ws, mw, gi = self.next_slab("win")
            for jj in range(4):
                j = (sl - 4) * 4 + jj
                ps = self.bank()
                for k in range(8):
                    p.op(PE, lambda e, ps=ps, k=k, jj=jj, ws=ws: e.matmul(
                        ps[:, 0:TT], ws[:, k, jj * 128:(jj + 1) * 128], hbf[:, k, :], start=(k == 0), stop=(k == 7)),
                        reads=[ws[:, k, jj * 128:(jj + 1) * 128], hbf[:, k, :]], writes=[ps])
                p.op(ACT, lambda e, ps=ps, j=j: e.activation(out=xbc[:, j, 3:3 + TT], in_=ps[:, 0:TT], func=AF.Copy),
                     reads=[ps], writes=[xbc[:, j, 3:3 + TT]])
                p.op(DVE, lambda e, ps=ps, j=j: e.tensor_copy(out=self.xl32[:, j, :], in_=ps[:, TT - 3:TT]),
                     reads=[ps], writes=[self.xl32[:, j, :]])
            self.slab_done(gi)
        ws, mw, gi = self.next_slab("win")
        assert mw == 32
        ps = self.bank()
        for k in range(8):
            p.op(PE, lambda e, ps=ps, k=k, ws=ws: e.matmul(ps[0:32, 0:TT], ws[:, k, 0:32], hbf[:, k, :], start=(k == 0), stop=(k == 7)),
                 reads=[ws[:, k, 0:32], hbf[:, k, :]], writes=[ps])
        self.slab_done(gi)
        v, av, ll = (t[0:32, :] for t in tv)
        dtb = self.vc("dtb")[0:32, :]
        p.op(DVE, lambda e, ps=ps: e.tensor_scalar(out=v, in0=ps[0:32, 0:TT], scalar1=dtb, scalar2=None, op0=ALU.add),
             reads=[ps, dtb], writes=[v])
        p.op(ACT, lambda e: e.activation(out=av, in_=v, func=AF.Abs), reads=[v], writes=[av])
        p.op(ACT, lambda e: e.activation(out=av, in_=av, func=AF.Exp, scale=-1.0), reads=[av], writes=[av])
        p.op(ACT, lambda e: e.activation(out=ll, in_=av, func=AF.Ln, bias=1.0), reads=[av], writes=[ll])
        p.op(DVE, lambda e: e.scalar_tensor_tensor(out=dtT[0:32, :], in0=v, scalar=0.0, in1=ll, op0=ALU.max, op1=ALU.add),
             reads=[v, ll], writes=[dtT[0:32, :]])
        p.op(DVE, lambda e: e.tensor_scalar(out=dtAT[0:32, :], in0=dtT[0:32, :], scalar1=self.acol[0:32, :], scalar2=None, op0=ALU.mult),
             reads=[dtT[0:32, :], self.acol[0:32, :]], writes=[dtAT[0:32, :]])
        for sl in range(4):
            ws, mw, gi = self.next_slab("dgs")
            for jj in range(6):
                j = 6 * sl + jj
                ps = self.bank()
                for k in range(4):
                    p.op(PE, lambda e, ps=ps, k=k, ws=ws, jj=jj, j=j: e.matmul(ps[:, 0:TT], ws[:, 4 * jj + k, :], xbc[:, j, k:k + TT],
                                                                        start=(k == 0), stop=(k == 3)),
                         reads=[ws[:, 4 * jj + k, :], xbc[:, j, k:k + TT]], writes=[ps])
                dst = xs32[:, j, :] if j < 16 else (BT[:, j - 16, :] if j < 20 else CT[:, j - 20, :])
                cb = self.vc("scb", 24)[:, j:j + 1]
                p.op(ACT, lambda e, ps=ps, dst=dst, cb=cb: e.activation(out=dst, in_=ps[:, 0:TT], func=AF.Silu, bias=cb),
                     reads=[ps, cb], writes=[dst])
            self.slab_done(gi)
        A.release(m1)
        dtokq = [A.alloc([64], F32) for _ in range(2)]
        dhl = A.alloc([64], BF16)
        rhs1h = A.alloc([16, L], BF16)
        rhs1l = A.alloc([16, L], BF16)
        decay = A.alloc([16, L], F32)
        scm = A.alloc([4, L], F32)
        wts2 = [A.alloc([32, L], BF16) for _ in range(2)]
        xdtp2 = [A.alloc([16, 2, 128], BF16) for _ in range(2)]
        xend2 = [A.alloc([32, 64], BF16) for _ in range(2)]
        btok2 = [A.alloc([4, 128], BF16) for _ in range(2)]
        ctok2 = [A.alloc([4, 128], BF16) for _ in range(2)]
        eeq = [A.alloc([64], F32) for _ in range(2)]
        eal2 = [A.alloc([nchk, 32], F32) for _ in range(2)]
        dgm2 = [A.alloc([32, L], BF16) for _ in range(2)]
        cexp = A.alloc([32, L], BF16)
        dcol = self.vc("dexp", 16)
        bdb, bdi, idl = self.bdb, self.bdi, self.idl
        trib, idlb = self.mskb[:, 0:64], self.mskb[:, 64:128]
        eab = [A.alloc([32], BF16) for _ in range(2)]
        for xd_ in xdtp2:
            p.op(DVE, lambda e, xd_=xd_: e.memset(xd_, 0.0), writes=[xd_])

        def xdtp_data(xdtp, rows, c0, ncx):
            a = xdtp[rows, c0:c0 + ncx, :, 0:64]
            return bass.AP(a.tensor, a.offset, [list(a.ap[0]), [256, ncx], [192, 2], [1, 64]])

        def bufs(q):
            i = q % 2
            return wts2[i], xdtp2[i], xend2[i], btok2[i], ctok2[i], eal2[i], dgm2[i]

        def prep_a(q):
            t0 = q * GT
            wts, xdtp, xend, btok, ctok, eal, dgm = bufs(q)
            dtok = dtokq[q % 2]
            ee = eeq[q % 2]
            bkt = self.bank()
            p.op(PE, lambda e, bkt=bkt, t0=t0: e.transpose(bkt[0:R, 0:32], dtT[0:32, t0:t0 + R], ident[0:32, 0:32]),
                 reads=[dtT[0:32, t0:t0 + R], ident[0:32, 0:32]], writes=[bkt])
            p.op(PE, lambda e, bkt=bkt, t0=t0: e.transpose(bkt[0:R, 32:64], dtAT[0:32, t0:t0 + R], ident[0:32, 0:32]),
                 reads=[dtAT[0:32, t0:t0 + R], ident[0:32, 0:32]], writes=[bkt])
            p.op(DVE, lambda e, bkt=bkt: e.tensor_copy(out=dtok[0:R, :], in_=bkt[0:R, 0:64]), reads=[bkt], writes=[dtok[0:R, :]])
            dta = dtok[0:R, 32:64]
            dhi, dlo = dhl[0:R, 0:32], dhl[0:R, 32:64]
            p.op(DVE, lambda e: e.tensor_copy(out=dhi, in_=dta), reads=[dta], writes=[dhi])
            p.op(DVE, lambda e: e.tensor_tensor(out=dlo, in0=dta, in1=dhi, op=ALU.subtract), reads=[dta, dhi], writes=[dlo])
            bke = self.bank()
            p.op(PE, lambda e, bke=bke: e.matmul(bke[0:R, 0:32], bd[0:R, 0:R], dta, start=True, stop=True),
                 reads=[bd[0:R, 0:R], dta], writes=[bke])
            p.op(PE, lambda e, bke=bke: e.matmul(bke[0:R, 32:64], bdi[0:R, 0:R], dta, start=True, stop=True),
                 reads=[bdi[0:R, 0:R], dta], writes=[bke])
            p.op(ACT, lambda e, bke=bke: e.activation(out=ee[0:R, :], in_=bke[0:R, 0:64], func=AF.Exp), reads=[bke], writes=[ee[0:R, :]])
            eend = ee[:, 0:32]
            eact = ee[:, 32:64]
            for j in range(nchk):
                rj = slice(64 * j, 64 * j + L)
                bkj = self.bank()
                p.op(PE, lambda e, bkj=bkj, rj=rj: e.matmul(bkj[:, 0:32], ones[rj, :], dtok[rj, 32:64], start=True, stop=True),
                     reads=[ones[rj, :], dtok[rj, 32:64]], writes=[bkj])
                p.op(ACT, lambda e, bkj=bkj, j=j: e.activation(out=eal[:, j, :], in_=bkj[:, 0:32], func=AF.Exp),
                     reads=[bkj], writes=[eal[:, j, :]])
            eb = eab[q % 2]
            p.op(DVE, lambda e: e.tensor_copy(out=eb[0:R, :], in_=eact[0:R, :]), reads=[eact[0:R, :]], writes=[eb[0:R, :]])
            p.op(DVE, lambda e: e.tensor_tensor(out=dgm[0:R], in0=bc(eb[0:R, :].unsqueeze(2), [R, 32, L]),
                                                in1=bc(idlb[0:R, 0:L].unsqueeze(1), [R, 32, L]), op=ALU.mult),
                 reads=[eb[0:R, :], idlb[0:R, 0:L]], writes=[dgm[0:R]])
            bkc = self.bank()
            for gq in range(4):
                p.op(PE, lambda e, bkc=bkc, gq=gq, t0=t0: e.matmul(bkc[0:R, gq * R:(gq + 1) * R], BT[:, gq, t0:t0 + R], CT[:, gq, t0:t0 + R],
                                                                  start=True, stop=True),
                     reads=[BT[:, gq, t0:t0 + R], CT[:, gq, t0:t0 + R]], writes=[bkc])
            for j in range(nchk):
                rj = slice(64 * j, 64 * j + L)
                p.op(DVE, lambda e, bkc=bkc, j=j, rj=rj: e.tensor_tensor(
                    out=scm[rj], in0=bkc[rj, 0:4 * R].rearrange("p (g l) -> p g l", g=4)[:, :, 64 * j:64 * j + L],
                    in1=bc(tri[rj, 0:L].unsqueeze(1), [L, 4, L]), op=ALU.mult), reads=[bkc, tri[rj, 0:L]], writes=[scm[rj]])
            for hh in range(2):
                for dsrc, rdst in ((dhi, rhs1h), (dlo, rhs1l)):
                    dsl = dsrc[:, 16 * hh:16 * hh + 16]
                    p.op(DVE, lambda e, dsl=dsl, rdst=rdst: e.tensor_tensor(out=rdst[0:R], in0=bc(dsl.unsqueeze(2), [R, 16, L]),
                                                                          in1=bc(trib[0:R, 0:L].unsqueeze(1), [R, 16, L]), op=ALU.mult),
                         reads=[dsl, trib[0:R, 0:L]], writes=[rdst[0:R]])
                for i2 in range(2):
                    bks = self.bank()
                    h0 = 8 * i2
                    for ri, rr in enumerate((rhs1h, rhs1l)):
                        p.op(PE, lambda e, bks=bks, h0=h0, rr=rr, ri=ri: e.matmul(bks[0:R, 0:8 * L], bdb[0:R, 0:R],
                                                                               rr[0:R, h0:h0 + 8, :].rearrange("p h l -> p (h l)"),
                                                                               start=(ri == 0), stop=(ri == 1)),
                             reads=[bdb[0:R, 0:R], rr[0:R, h0:h0 + 8, :]], writes=[bks])
                    p.op(ACT, lambda e, bks=bks, i2=i2: e.activation(out=decay[0:R, 8 * i2:8 * i2 + 8, :],
                                                                     in_=bks[0:R, 0:8 * L].rearrange("p (h l) -> p h l", h=8), func=AF.Exp),
                         reads=[bks], writes=[decay[0:R, 8 * i2:8 * i2 + 8, :]])
                p.op(DVE, lambda e, hh=hh: e.tensor_tensor(
                    out=wts[0:R, 16 * hh:16 * hh + 16, :].rearrange("p (g h) l -> p g h l", g=2),
                    in0=decay[0:R].rearrange("p (g h) l -> p g h l", g=2),
                    in1=bc(scm[0:R, 2 * hh:2 * hh + 2, :].unsqueeze(2), [R, 2, 8, L]), op=ALU.mult),
                    reads=[decay[0:R], scm[0:R, 2 * hh:2 * hh + 2, :]], writes=[wts[0:R, 16 * hh:16 * hh + 16, :]])

        def prep_b(q):
            t0 = q * GT
            wts, xdtp, xend, btok, ctok, eal, dgm = bufs(q)
            eend = eeq[q % 2][:, 0:32]
            dtok = dtokq[q % 2]
            for b4 in range(4):
                bkx = self.bank()
                for i in range(4):
                    c = 4 * b4 + i
                    src = xs32[:, c, t0:t0 + R]
                    p.op(PE, lambda e, bkx=bkx, i=i, src=src: e.transpose(bkx[0:R, i * 128:(i + 1) * 128], src, ident),
                         reads=[src, ident], writes=[bkx])
                dstv = xdtp_data(xdtp, slice(0, R), 4 * b4, 4)
                p.op(DVE, lambda e, bkx=bkx, dstv=dstv, b4=b4: e.tensor_tensor(
                    out=dstv, in0=bkx[0:R, :].rearrange("p (c e q) -> p c e q", c=4, e=2),
                    in1=bc(dtok[0:R, 8 * b4:8 * b4 + 8].rearrange("p (c e) -> p c e", c=4).unsqueeze(3), [R, 4, 2, 64]), op=ALU.mult),
                    reads=[bkx, dtok[0:R, 8 * b4:8 * b4 + 8]], writes=[xdtp[0:R, 4 * b4:4 * b4 + 4]])
            p.op(DVE, lambda e: e.tensor_tensor(
                out=xend[0:R].rearrange("p (c e) q -> p c e q", e=2), in0=xdtp_data(xdtp, slice(0, R), 0, 16),
                in1=bc(eend[0:R, :].rearrange("p (c e) -> p c e", e=2).unsqueeze(3), [R, 16, 2, 64]), op=ALU.mult),
                reads=[xdtp[0:R], eend[0:R, :]], writes=[xend[0:R]])
            xg = xs32[:, :, t0:t0 + R]
            p.op(DVE, lambda e, xg=xg: e.tensor_tensor(out=xg, in0=xg, in1=bc(dcol.unsqueeze(2), [128, 16, R]), op=ALU.mult),
                 reads=[xg, dcol], writes=[xg])
            for srcT, dtk in ((BT, btok), (CT, ctok)):
                bkb = self.bank()
                for gq in range(4):
                    src = srcT[:, gq, t0:t0 + R]
                    p.op(PE, lambda e, bkb=bkb, gq=gq, src=src: e.transpose(bkb[0:R, gq * 128:(gq + 1) * 128], src, ident),
                         reads=[src, ident], writes=[bkb])
                p.op(ACT, lambda e, bkb=bkb, dtk=dtk: e.activation(out=dtk[0:R], in_=bkb[0:R, :].rearrange("p (g n) -> p g n", g=4), func=AF.Copy),
                     reads=[bkb], writes=[dtk[0:R]])
        def chunk(q, j):
            t0 = q * GT
            wts, xdtp, xend, btok, ctok, eal, dgm = bufs(q)
            rj = slice(64 * j, 64 * j + L)
            tj = t0 + 64 * j
            for gq in range(4):
                bka = self.bank()
                p.op(PE, lambda e, bka=bka, gq=gq, rj=rj: e.matmul(bka[:, 0:8 * L], ctok[rj, gq, :],
                                                                  dgm[rj, 8 * gq:8 * gq + 8, :].rearrange("p h l -> p (h l)"), start=True, stop=True),
                     reads=[ctok[rj, gq, :], dgm[rj, 8 * gq:8 * gq + 8, :]], writes=[bka])
                p.op(ACT, lambda e, bka=bka, gq=gq: e.activation(out=cexp[:, 8 * gq:8 * gq + 8, :],
                                                                 in_=bka[:, 0:8 * L].rearrange("p (h l) -> p h l", h=8), func=AF.Copy),
                     reads=[bka], writes=[cexp[:, 8 * gq:8 * gq + 8, :]])
            for hh in range(2):
                bky = self.bank()
                for cc in range(8):
                    c = 8 * hh + cc
                    reg = bky[:, cc * L:(cc + 1) * L]
                    for e2 in range(2):
                        p.op(PE, lambda e, reg=reg, c=c, e2=e2, rj=rj: e.matmul(reg, xdtp[rj, c, e2, :], wts[rj, 2 * c + e2, :],
                                                                               start=(e2 == 0), stop=False),
                             reads=[xdtp[rj, c, e2, :], wts[rj, 2 * c + e2, :]], writes=[bky])
                    for e2 in range(2):
                        p.op(PE, lambda e, reg=reg, c=c, e2=e2: e.matmul(reg, Hb[:, c, e2, :], cexp[:, 2 * c + e2, :],
                                                                        start=False, stop=(e2 == 1)),
                             reads=[Hb[:, c, e2, :], cexp[:, 2 * c + e2, :]], writes=[bky])
                yv = xs32[:, 8 * hh:8 * hh + 8, tj:tj + L]
                p.op(DVE, lambda e, bky=bky, yv=yv: e.tensor_tensor(out=yv, in0=bky[:, 0:8 * L].rearrange("p (c l) -> p c l", c=8),
                                                                    in1=yv, op=ALU.add), reads=[bky, yv], writes=[yv])
            for gq in range(4):
                hv = H[:, 8 * gq:8 * gq + 8, :]
                p.op(DVE, lambda e, hv=hv, gq=gq, j=j: e.tensor_tensor(out=hv, in0=hv, in1=bc(eal[:, j, 8 * gq:8 * gq + 8].unsqueeze(2), [128, 8, HP]),
                                                                      op=ALU.mult), reads=[hv, eal[:, j, 8 * gq:8 * gq + 8]], writes=[hv])
                bkh = self.bank()
                p.op(PE, lambda e, bkh=bkh, gq=gq, rj=rj: e.matmul(bkh[:, 0:512], btok[rj, gq, :],
                                                                  xend[rj, 8 * gq:8 * gq + 8, :].rearrange("p h q -> p (h q)"), start=True, stop=True),
                     reads=[btok[rj, gq, :], xend[rj, 8 * gq:8 * gq + 8, :]], writes=[bkh])
                p.op(DVE, lambda e, bkh=bkh, hv=hv: e.tensor_tensor(out=hv, in0=bkh[:, 0:512].rearrange("p (h q) -> p h q", h=8),
                                                                    in1=hv, op=ALU.add), reads=[bkh, hv], writes=[hv])
                self.refresh_Hb(gq)

        prep_a(0)
        prep_b(0)
        for q in range(ngrp):
            for j in range(nchk):
                if q + 1 < ngrp:
                    if j == 0:
                        prep_a(q + 1)
                    if j == nchk - 1:
                        prep_b(q + 1)
                chunk(q, j)
        A.release(m1)
        szt = [A.alloc([TT], F32) for _ in range(2)]
        gacc = self.acc_new(self.sbank, 16, TT, True)
        for sl in range(4):
            ws, mw, gi = self.next_slab("win")
            for jj in range(4):
                j = sl * 4 + jj
                ps = self.bank()
                for k in range(8):
                    p.op(PE, lambda e, ps=ps, k=k, jj=jj, ws=ws: e.matmul(
                        ps[:, 0:TT], ws[:, k, jj * 128:(jj + 1) * 128], hbf[:, k, :], start=(k == 0), stop=(k == 7)),
                        reads=[ws[:, k, jj * 128:(jj + 1) * 128], hbf[:, k, :]], writes=[ps])
                sz = szt[j % 2]
                p.op(ACT, lambda e, ps=ps, sz=sz: e.activation(out=sz, in_=ps[:, 0:TT], func=AF.Silu), reads=[ps], writes=[sz])
                yj = xs32[:, j, :]
                p.op(DVE, lambda e, sz=sz, yj=yj: e.tensor_tensor(out=yj, in0=yj, in1=sz, op=ALU.mult), reads=[yj, sz], writes=[yj])
                self.acc_push(gacc, yj, j)
            self.slab_done(gi)
        ybf = A.alloc([16, TT], BF16)
        self.rms(xs32, 16, TT, self.vc("sng", 16), None, ybf, pre_stats=True)
        gt2 = self.modc[:, 0, 40:48, g]
        for sl in range(4):
            ws, mw, gi = self.next_slab("wout")
            for mm in range(2):
                m = 2 * sl + mm
                ps = self.bank()
                for k in range(16):
                    p.op(PE, lambda e, ps=ps, k=k, mm=mm, ws=ws: e.matmul(
                        ps[:, 0:TT], ws[:, k, mm * 128:(mm + 1) * 128], ybf[:, k, :], start=(k == 0), stop=(k == 15)),
                        reads=[ws[:, k, mm * 128:(mm + 1) * 128], ybf[:, k, :]], writes=[ps])
                xm = xres[:, m, :]
                p.op(DVE, lambda e, ps=ps, xm=xm, m=m: e.scalar_tensor_tensor(
                    out=xm, in0=ps[:, 0:TT], scalar=gt2[:, m:m + 1], in1=xm, op0=ALU.mult, op1=ALU.add),
                    reads=[ps, gt2[:, m:m + 1], xm], writes=[xm])
                self.stat_push(TT, m)
            self.slab_done(gi)
        A.release(m0)

    def convmod(self, TT, g):
        p, A, ident, ones = self.p, self.A, self.ident, self.ones
        xres = self.xres[:, :, 0:TT]
        HK = KW - 1
        m0 = A.mark()
        hbf = A.alloc([8, TT], BF16)
        self.rms(xres, 8, TT, self.gsc(g, 1, 1), self.shc(g, 1, 1), hbf, pre_stats=True)
        u32 = A.alloc([8, TT + HK], F32)
        ub = A.alloc([8, TT + HK], BF16)
        sg = [A.alloc([TT], F32) for _ in range(2)]
        p.op(DVE, lambda e: e.tensor_copy(out=u32[:, :, 0:HK], in_=self.uhist[:]), reads=[self.uhist[:]], writes=[u32[:, :, 0:HK]])
        bp1 = self.vc("bp1", 16)
        for sl in range(4):
            ws, mw, gi = self.next_slab("pw1")
            for jj in range(4):
                j = sl * 4 + jj
                c = j % 8
                ps = self.bank()
                for k in range(8):
                    p.op(PE, lambda e, ps=ps, k=k, jj=jj, ws=ws: e.matmul(
                        ps[:, 0:TT], ws[:, k, jj * 128:(jj + 1) * 128], hbf[:, k, :], start=(k == 0), stop=(k == 7)),
                        reads=[ws[:, k, jj * 128:(jj + 1) * 128], hbf[:, k, :]], writes=[ps])
                uc = u32[:, c, HK:HK + TT]
                bj = bp1[:, j:j + 1]
                if j < 8:
                    p.op(ACT, lambda e, ps=ps, uc=uc, bj=bj: e.activation(out=uc, in_=ps[:, 0:TT], func=AF.Identity, bias=bj),
                         reads=[ps, bj], writes=[uc])
                else:
                    s_ = sg[j % 2]
                    p.op(ACT, lambda e, ps=ps, s_=s_, bj=bj: e.activation(out=s_, in_=ps[:, 0:TT], func=AF.Sigmoid, bias=bj),
                         reads=[ps, bj], writes=[s_])
                    p.op(DVE, lambda e, uc=uc, s_=s_: e.tensor_tensor(out=uc, in0=uc, in1=s_, op=ALU.mult), reads=[uc, s_], writes=[uc])
            self.slab_done(gi)
        p.op(DVE, lambda e: e.tensor_copy(out=self.uhist[:], in_=u32[:, :, TT:TT + HK]), reads=[u32[:, :, TT:TT + HK]], writes=[self.uhist[:]])
        p.op(ACT, lambda e: e.activation(out=ub, in_=u32, func=AF.Copy), reads=[u32], writes=[ub])
        v32 = A.alloc([8, TT], F32)
        dwb = self.vc("dwb", 8)
        macc = self.acc_new(self.mbank, 8, TT, False)
        vacc = self.acc_new(self.sbank, 8, TT, True)
        for c in range(8):
            d, mw, gi = self.next_slab("dgc")
            ps = self.bank()
            for k in range(KW):
                p.op(PE, lambda e, ps=ps, k=k, d=d, c=c: e.matmul(ps[:, 0:TT], d[:, k, :], ub[:, c, k:k + TT], start=(k == 0), stop=(k == KW - 1)),
                     reads=[d[:, k, :], ub[:, c, k:k + TT]], writes=[ps])
            self.slab_done(gi)
            p.op(ACT, lambda e, ps=ps, c=c: e.activation(out=v32[:, c, :], in_=ps[:, 0:TT], func=AF.Identity, bias=dwb[:, c:c + 1]),
                 reads=[ps, dwb[:, c:c + 1]], writes=[v32[:, c, :]])
            self.acc_push(macc, v32[:, c, :], c)
            self.acc_push(vacc, v32[:, c, :], c)
        mean = A.alloc([TT], F32)
        m2 = A.alloc([TT], F32)
        rstd = A.alloc([TT], F32)
        p.op(ACT, lambda e: e.activation(out=mean, in_=self.mbank[:, 0:TT], func=AF.Identity, scale=1.0 / D), reads=[self.mbank], writes=[mean])
        p.op(DVE, lambda e: e.tensor_tensor(out=m2, in0=mean, in1=mean, op=ALU.mult), reads=[mean], writes=[m2])
        p.op(DVE, lambda e: e.scalar_tensor_tensor(out=rstd, in0=self.sbank[:, 0:TT], scalar=1.0 / D, in1=m2, op0=ALU.mult, op1=ALU.subtract),
             reads=[self.sbank, m2], writes=[rstd])
        p.op(ACT, lambda e: e.activation(out=rstd, in_=rstd, func=AF.Sqrt, bias=EPS), reads=[rstd], writes=[rstd])
        p.op(DVE, lambda e: e.reciprocal(out=rstd, in_=rstd), reads=[rstd], writes=[rstd])
        hb2 = A.alloc([8, TT], BF16)
        lng, lnb = self.vc("lng", 8), self.vc("lnb", 8)
        for h0 in range(0, 8, 2):
            vq = v32[:, h0:h0 + 2, :]
            p.op(DVE, lambda e, vq=vq: e.tensor_tensor(out=vq, in0=vq, in1=bc(mean.unsqueeze(1), [128, 2, TT]), op=ALU.subtract),
                 reads=[vq, mean], writes=[vq])
            p.op(DVE, lambda e, vq=vq: e.tensor_tensor(out=vq, in0=vq, in1=bc(rstd.unsqueeze(1), [128, 2, TT]), op=ALU.mult),
                 reads=[vq, rstd], writes=[vq])
            for c in range(h0, h0 + 2):
                p.op(ACT, lambda e, c=c: e.activation(out=hb2[:, c, :], in_=v32[:, c, :], func=AF.Silu, bias=lnb[:, c:c + 1], scale=lng[:, c:c + 1]),
                     reads=[v32[:, c, :], lnb[:, c:c + 1], lng[:, c:c + 1]], writes=[hb2[:, c, :]])
        gt2 = self.modc[:, 1, 40:48, g]
        bg = self.der[:, g, 1, 5, :]
        tt = [A.alloc([TT], F32) for _ in range(2)]
        for sl in range(2):
            ws, mw, gi = self.next_slab("pw2")
            for mm in range(4):
                m = 4 * sl + mm
                ps = self.bank()
                for k in range(8):
                    p.op(PE, lambda e, ps=ps, k=k, mm=mm, ws=ws: e.matmul(
                        ps[:, 0:TT], ws[:, k, mm * 128:(mm + 1) * 128], hb2[:, k, :], start=(k == 0), stop=(k == 7)),
                        reads=[ws[:, k, mm * 128:(mm + 1) * 128], hb2[:, k, :]], writes=[ps])
                t_ = tt[m % 2]
                p.op(ACT, lambda e, ps=ps, t_=t_, m=m: e.activation(out=t_, in_=ps[:, 0:TT], func=AF.Identity,
                                                                    bias=bg[:, m:m + 1], scale=gt2[:, m:m + 1]),
                     reads=[ps, bg[:, m:m + 1], gt2[:, m:m + 1]], writes=[t_])
                xm = xres[:, m, :]
                p.op(DVE, lambda e, t_=t_, xm=xm: e.tensor_tensor(out=xm, in0=xm, in1=t_, op=ALU.add), reads=[xm, t_], writes=[xm])
                self.stat_push(TT, m)
            self.slab_done(gi)
        A.release(m0)

    def tile(self, src, dst, r0, TT, g, dbg=False, nxt=None, first=True):
        A = self.A
        if self.budget <= 0:
            return
        if first:
            self.load_x(src, r0, TT)
        stage = [0]
        steps = [lambda: self.ffn(TT, g, 0, 0, 0), lambda: self.ssd(TT, g), lambda: self.ffn(TT, g, 0, 2, 1),
                 lambda: self.ffn(TT, g, 1, 0, 0), lambda: self.convmod(TT, g),
                 lambda: ((self.prefetch_x(*nxt) if nxt else None), self.ffn(TT, g, 1, 2, 1))]

        def dump():
            if dbg and self.dbg:
                k = stage[0]
                self.p.dma(POOL, self.dbgo[k, :, :, 0:TT], self.xres[:, :, 0:TT], key="dbg")
                stage[0] += 1
        dump()
        for st in steps:
            if self.budget <= 0:
                return
            self.budget -= 1
            st()
            dump()
        if self.budget <= 0:
            return
        self.budget -= 1
        m0 = A.mark()
        yT = A.alloc([8, TT], F32)
        self.rms(self.xres[:, :, 0:TT], 8, TT, self.vc("fg", 8), None, yT, pre_stats=True,
                 mid=((lambda: self.load_x(*nxt)) if nxt else None))
        self.store_y(yT, dst, r0, TT)
        A.release(m0)

    def build(self, TT=512, budget=10 ** 9):
        S = self.S
        self.stgn = 0
        self.budget = budget
        self.setup()
        self.cast_weights()
        nt = S // TT
        self.start_wstream(nt + 1)
        self.init_state_zero()
        for t in range(nt):
            nxt = (self.xp, (t + 1) * TT, TT) if t + 1 < nt else None
            self.tile(self.xp, self.yp, t * TT, TT, 0, dbg=(t == 0), nxt=nxt, first=(t == 0))
        if self.budget > 0:
            self.store_states(0)
            self.budget -= 1
        if self.budget > 0:
            self.load_sample_state()
            self.budget -= 1
            self.enter_sample_mode()
        self.tile(self.xs, self.ys, 0, DEC, 1)
        if self.budget > 0:
            self.store_states(1)
        for q in (SP, ACT, POOL):
            self.p.barrier_all_dma(q)
        self.p.emit(self.nc, self.stack)
        self.stack.close()
        return self.nc


def make_in_maps(inputs, S=SEQ, ncores=NCORES):
    vecs, _ = _vec_rows(inputs)
    consts = _consts()
    shared = {
        "consts": consts, "vecs": vecs,
        "mod_w": np.ascontiguousarray(inputs["mod_w"], np.float32),
        "ffn_w1": np.ascontiguousarray(inputs["ffn_w1"], np.float32),
        "ffn_w3": np.ascontiguousarray(inputs["ffn_w3"], np.float32),
        "ffn_w2": np.ascontiguousarray(inputs["ffn_w2"], np.float32),
        "ssd_w_in": np.ascontiguousarray(inputs["ssd_w_in"], np.float32),
        "ssd_w_out": np.ascontiguousarray(inputs["ssd_w_out"], np.float32),
        "cmod_w_pw1": np.ascontiguousarray(inputs["cmod_w_pw1"], np.float32),
        "cmod_w_pw2": np.ascontiguousarray(inputs["cmod_w_pw2"], np.float32),
    }
    maps = []
    for b in range(ncores):
        m = dict(shared)
        m["xp"] = np.ascontiguousarray(inputs["x_prompt"][b, :S], np.float32)
        m["xs"] = np.ascontiguousarray(inputs["x_sample"][b], np.float32)
        m["c2"] = np.ascontiguousarray(np.stack([inputs["c_prompt"][b], inputs["c_sample"][b]]), np.float32)
        m["st_in"] = np.ascontiguousarray(inputs["state_ssd"][0, b], np.float32).reshape(NH * HP, NST)
        m["sc_in"] = np.ascontiguousarray(inputs["cache_ssd_conv"][0, b], np.float32)
        m["cc_in"] = np.ascontiguousarray(inputs["cache_cmod_conv"][0, b], np.float32)
        maps.append(m)
    return maps


def run(inputs, S=SEQ, ncores=NCORES, dbg=False, trace=False, budget=10 ** 9):
    inputs = {k: np.asarray(v) for k, v in inputs.items()}
    b = Builder(S, dbg=dbg)
    nc = b.build(budget=budget)
    maps = make_in_maps(inputs, S, ncores)
    kw = {"trace": True} if trace else {}
    res = run_bass_kernel_spmd(nc, maps, core_ids=list(range(ncores)), **kw)
    return res, b


def kernel(**inputs):
    res, _ = run(inputs)
    r = res.results
    n = NCORES
    yp = np.stack([r[b]["yp"] for b in range(n)]).astype(np.float32)
    ys = np.stack([r[b]["ys"] for b in range(n)]).astype(np.float32)

    def st(name):
        return np.stack([r[b][name].reshape(NH, HP, NST) for b in range(n)])[None].astype(np.float32)

    def plain(name):
        return np.stack([r[b][name] for b in range(n)])[None].astype(np.float32)

    return (yp, ys, st("st_p"), plain("sc_p"), plain("cc_p"), st("st_s"), plain("sc_s"), plain("cc_s"))
```

```python
from contextlib import ExitStack
import numpy as np
import concourse.bass as bass
import concourse.mybir as mybir
from concourse.bass_utils import run_bass_kernel_spmd

F32 = mybir.dt.float32
BF16 = mybir.dt.bfloat16
AF = mybir.ActivationFunctionType
ALU = mybir.AluOpType

D = 1024
DFF = 2816
DIN = 2048
NH = 32
HP = 64
NST = 128
CONVD = 3072
PROJ = 5152
KW = 31
EPS = 1e-6
SEQ = 8192
DEC = 16
NCORES = 8

PE, ACT, DVE, POOL, SP = "pe", "act", "dve", "pool", "sp"
ENGS = [PE, ACT, DVE, POOL, SP]
EPOCH = 30000


def _esz(dt):
    return 2 if dt == BF16 else 4


class Op:
    __slots__ = ("eng", "fn", "deps", "is_dma", "key", "count", "tickno", "needed", "idx", "dma_wait")

    def __init__(self, eng, fn):
        self.eng = eng
        self.fn = fn
        self.deps = []
        self.is_dma = False
        self.key = None
        self.count = 0
        self.tickno = 0
        self.needed = False
        self.dma_wait = {}


class Prog:
    def __init__(self):
        self.q = {e: [] for e in ENGS}
        self.recs = {}
        self.dma_counts = {}
        self.ordered_keys = {f"cast{i}" for i in range(8)}
        self.nops = 0

    @staticmethod
    def region(ap):
        if not hasattr(ap, "tensor"):
            ap = ap[:]
        t = ap.tensor
        name = t.name
        esz = _esz(ap.dtype)
        dims = ap.ap
        off = int(ap.offset)
        kind = type(t).__name__
        if kind.startswith("DRam"):
            ext = sum(abs(s) * (c - 1) for s, c in dims) + 1
            return name, 0, 1, off * esz, (off + ext) * esz, "dram"
        if kind.startswith("PSum"):
            return name, 0, 128, 0, 1 << 30, "psum"
        row = dims[0][0]
        p0 = off // row
        c0 = off % row
        p1 = p0 + dims[0][1]
        ext = sum(abs(s) * (c - 1) for s, c in dims[1:]) + 1
        return name, p0, p1, c0 * esz, (c0 + ext) * esz, "sbuf"

    def _access(self, op, ap, is_write):
        name, p0, p1, b0, b1, kind = self.region(ap)
        lst = self.recs.setdefault(name, [])
        psum = kind == "psum"
        wr = is_write or psum
        keep = []
        hit = False
        for r in lst:
            if r[0] < p1 and p0 < r[1] and r[2] < b1 and b0 < r[3]:
                hit = True
                w = r[4]
                if w is not None and w is not op:
                    op.deps.append((w, "raw" if not is_write else "waw"))
                if wr:
                    for rd in r[5]:
                        if rd is not op:
                            op.deps.append((rd, "war"))
                    x0, x1 = max(r[2], b0), min(r[3], b1)
                    if r[2] < x0:
                        keep.append([r[0], r[1], r[2], x0, r[4], list(r[5])])
                    if x1 < r[3]:
                        keep.append([r[0], r[1], x1, r[3], r[4], list(r[5])])
                    if r[0] < p0:
                        keep.append([r[0], p0, x0, x1, r[4], list(r[5])])
                    if p1 < r[1]:
                        keep.append([p1, r[1], x0, x1, r[4], list(r[5])])
                    continue
                r[5].append(op)
            keep.append(r)
        if wr:
            keep.append([p0, p1, b0, b1, op, []])
        elif not hit:
            keep.append([p0, p1, b0, b1, None, [op]])
        self.recs[name] = keep

    def _finish(self, op, reads, writes):
        for ap in reads:
            self._access(op, ap, False)
        for ap in writes:
            self._access(op, ap, True)
        deps = []
        seen = set()
        for d, kind in op.deps:
            if id(d) in seen:
                continue
            if d.is_dma:
                pass
            elif d.eng == op.eng and not op.is_dma:
                if op.eng == PE:
                    continue
            seen.add(id(d))
            deps.append(d)
            if d.is_dma:
                cnt = d.count if d.key in self.ordered_keys else max(self.dma_counts[d.key], d.count)
                op.dma_wait[d.key] = max(op.dma_wait.get(d.key, 0), cnt)
            else:
                d.needed = True
        op.deps = deps
        op.idx = self.nops
        self.nops += 1
        self.q[op.eng].append(op)
        return op

    def op(self, eng, fn, reads=(), writes=()):
        return self._finish(Op(eng, fn), list(reads), list(writes))

    def dma(self, queue, out, in_, key):
        o = Op(queue, None)
        o.is_dma = True
        o.key = key
        self.dma_counts.setdefault(key, 0)
        o.fn = lambda e, out=out, in_=in_: e.dma_start(out=out, in_=in_)
        self._finish(o, [in_], [out])
        if key in self.ordered_keys and self.dma_counts[key] > 0:
            o.dma_wait[key] = self.dma_counts[key]
        self.dma_counts[key] += 16
        o.count = self.dma_counts[key]
        return o

    def barrier_all_dma(self, queue):
        o = Op(queue, None)
        o.fn = None
        o.dma_wait = dict(self.dma_counts)
        o.idx = self.nops
        self.nops += 1
        self.q[queue].append(o)

    def emit(self, nc, stack):
        engobj = {PE: "tensor", ACT: "scalar", DVE: "vector", POOL: "gpsimd", SP: "sync"}
        nt = {}
        for e in ENGS:
            n = 0
            for o in self.q[e]:
                if o.needed and not o.is_dma:
                    n += 1
                    o.tickno = n
            nt[e] = n
        esems = {}
        for e in ENGS:
            ne = max(1, (nt[e] + EPOCH - 1) // EPOCH)
            esems[e] = [stack.enter_context(nc.semaphore(f"t_{e}_{i}")) for i in range(ne)]
        dsems = {k: stack.enter_context(nc.semaphore(f"d_{k}")) for k in self.dma_counts}
        for k, v in self.dma_counts.items():
            assert v < 32000, (k, v)
        block = stack.enter_context(nc.Block())

        def make(e):
            ops = self.q[e]

            def body(eng):
                waited = {}
                for o in ops:
                    need = {}
                    for d in o.deps:
                        if d.is_dma:
                            continue
                        if need.get(d.eng, 0) < d.tickno:
                            need[d.eng] = d.tickno
                    for de, t in need.items():
                        if waited.get(de, 0) >= t:
                            continue
                        waited[de] = t
                        eng.wait_ge(esems[de][(t - 1) // EPOCH], (t - 1) % EPOCH + 1)
                    for k, v in o.dma_wait.items():
                        if waited.get(("dma", k), 0) >= v:
                            continue
                        waited[("dma", k)] = v
                        eng.wait_ge(dsems[k], v)
                    if o.fn is None:
                        continue
                    ins = o.fn(eng)
                    if o.is_dma:
                        ins.then_inc(dsems[o.key], 16)
                    elif o.tickno:
                        ins.then_inc(esems[e][(o.tickno - 1) // EPOCH], 1)
            return body

        for e in ENGS:
            getattr(block, engobj[e])(make(e))


def _vec_rows(inp):
    items = []

    def add(name, v):
        v = np.asarray(v, np.float32).reshape(-1)
        n = (v.size + 127) // 128 * 128
        if n != v.size:
            v = np.concatenate([v, np.zeros(n - v.size, np.float32)])
        items.append((name, v.reshape(-1, 128)))

    for i in range(2):
        for s in range(3):
            add(f"ng{i}{s}", inp["norm_g"][i, s])
    add("fg", inp["final_g"])
    for i in range(2):
        add(f"mb{i}", inp["mod_b"][i])
    for k in range(4):
        add(f"scw{k}", inp["ssd_conv_w"][0, k])
    add("scb", inp["ssd_conv_b"][0])
    add("sng", inp["ssd_norm_g"][0])
    add("dexp", np.repeat(inp["ssd_d"][0], HP))
    add("dtb", inp["ssd_dt_bias"][0])
    add("alog", inp["ssd_a_log"][0])
    add("bp1", inp["cmod_b_pw1"][0])
    for k in range(KW):
        add(f"dw{k}", inp["cmod_dw_w"][0, k])
    add("dwb", inp["cmod_dw_b"][0])
    add("lng", inp["cmod_ln_g"][0])
    add("lnb", inp["cmod_ln_b"][0])
    add("bp2", inp["cmod_b_pw2"][0])
    rows = {}
    r = 0
    mats = []
    for name, m in items:
        rows[name] = r
        r += m.shape[0]
        mats.append(m)
    tot = (r + 127) // 128 * 128
    mats.append(np.zeros((tot - r, 128), np.float32))
    return np.ascontiguousarray(np.concatenate(mats, 0)), rows


_VROWS = None


def _vrows_static():
    global _VROWS
    if _VROWS is None:
        fake = {
            "norm_g": np.zeros((2, 3, D)), "final_g": np.zeros(D), "mod_b": np.zeros((2, 9 * D)),
            "ssd_conv_w": np.zeros((1, 4, CONVD)), "ssd_conv_b": np.zeros((1, CONVD)),
            "ssd_norm_g": np.zeros((1, DIN)), "ssd_d": np.zeros((1, NH)), "ssd_dt_bias": np.zeros((1, NH)),
            "ssd_a_log": np.zeros((1, NH)), "cmod_b_pw1": np.zeros((1, 2 * D)),
            "cmod_dw_w": np.zeros((1, KW, D)), "cmod_dw_b": np.zeros((1, D)), "cmod_ln_g": np.zeros((1, D)),
            "cmod_ln_b": np.zeros((1, D)), "cmod_b_pw2": np.zeros((1, D)),
        }
        v, rows = _vec_rows(fake)
        _VROWS = (v.shape[0], rows)
    return _VROWS


def _consts():
    c = np.zeros((128, 640), np.float32)
    c[:, 0:128] = np.eye(128)
    c[:, 128:256] = 1.0
    p = np.arange(128)[:, None]
    l = np.arange(64)[None, :]
    c[:, 256:320] = ((p % 64) <= l)
    q = np.arange(128)[None, :]
    c[:, 320:448] = ((p // 64) == (q // 64)) & (p > q)
    c[:, 448:576] = ((p // 64) == (q // 64)) & (p <= q)
    c[:, 576:640] = ((p % 64) == l)
    return c


NSLOT = 4
SLOTW = 4096
ARENA_WORDS = 30500


class Arena:
    def __init__(self, t):
        self.t = t
        self.off = 0
        self.peak = 0

    def mark(self):
        return self.off

    def release(self, m):
        self.off = m

    def alloc(self, shape, dt):
        n = int(np.prod(shape))
        words = n if dt == F32 else (n + 1) // 2
        words = (words + 7) // 8 * 8
        assert self.off + words <= ARENA_WORDS, ("arena overflow", self.off, words)
        v = self.t[:, self.off:self.off + words]
        self.off += words
        self.peak = max(self.peak, self.off)
        if dt == BF16:
            v = v.bitcast(BF16)
        v = v[:, :n]
        if len(shape) == 2:
            v = v.rearrange("p (a b) -> p a b", a=shape[0])
        elif len(shape) == 3:
            v = v.rearrange("p (a b c) -> p a b c", a=shape[0], b=shape[1])
        return v


def bc(ap, shape):
    return ap.to_broadcast(list(shape))


class Builder:
    def __init__(self, S, dbg=False):
        self.S = S
        self.nc = nc = bass.Bass("TRN2", target_bir_lowering=False)
        self.p = Prog()
        self.stack = ExitStack()
        self.dbg = dbg
        nvr, self.vr = _vrows_static()
        self.nvr = nvr

        def din(name, shape):
            return nc.dram_tensor(name, list(shape), F32, kind="ExternalInput").ap()

        def dout(name, shape):
            return nc.dram_tensor(name, list(shape), F32, kind="ExternalOutput").ap()

        self.xp = din("xp", [S, D])
        self.xs = din("xs", [DEC, D])
        self.c2 = din("c2", [2, D])
        self.st_in = din("st_in", [NH * HP, NST])
        self.sc_in = din("sc_in", [3, CONVD])
        self.cc_in = din("cc_in", [KW - 1, D])
        self.consts = din("consts", [128, 640])
        self.vecs = din("vecs", [nvr, 128])
        self.mod_w = din("mod_w", [2, D, 9 * D])
        self.w = {
            "w1": din("ffn_w1", [2, 2, D, DFF]), "w3": din("ffn_w3", [2, 2, D, DFF]),
            "w2": din("ffn_w2", [2, 2, DFF, D]), "win": din("ssd_w_in", [1, D, PROJ]),
            "wout": din("ssd_w_out", [1, DIN, D]), "pw1": din("cmod_w_pw1", [1, D, 2 * D]),
            "pw2": din("cmod_w_pw2", [1, D, D]),
        }
        self.yp = dout("yp", [S, D])
        self.ys = dout("ys", [DEC, D])
        self.o_st = [dout("st_p", [NH * HP, NST]), dout("st_s", [NH * HP, NST])]
        self.o_sc = [dout("sc_p", [3, CONVD]), dout("sc_s", [3, CONVD])]
        self.o_cc = [dout("cc_p", [KW - 1, D]), dout("cc_s", [KW - 1, D])]
        if dbg:
            self.dbgo = dout("dbg", [8, 128, 8, 512])

        st = self.stack

        def sb(name, shape, dt=F32):
            return st.enter_context(nc.sbuf_tensor(name, list(shape), dt))

        self.cst = sb("cst", [128, 640])
        self.bdb = sb("bdb", [128, 128], BF16)
        self.mskb = sb("mskb", [128, 128], BF16)
        self.ident = self.cst[:, 0:128]
        self.ones = self.cst[:, 128:256]
        self.tri = self.cst[:, 256:320]
        self.bd = self.cst[:, 320:448]
        self.bdi = self.cst[:, 448:576]
        self.idl = self.cst[:, 576:640]
        self.vcol = sb("vcol", [128, nvr])
        self.modc = sb("modc", [128, 2, 72, 2])
        self.der = sb("der", [128, 2, 2, 6, 8])
        self.acol = sb("acol", [128, 1])
        self.xres = sb("xres", [128, 8, 512])
        self.H = sb("H", [128, NH, HP])
        self.Hb = sb("Hb", [128, 16, 2, 128], BF16)
        self.xl32 = sb("xl32", [128, 24, 3])
        self.uhist = sb("uhist", [128, 8, KW - 1])
        self.stg_in = [sb(f"stgi{i}", [128, D]) for i in range(2)]
        self.ring = [sb(f"ring{i}", [128, SLOTW], BF16) for i in range(NSLOT)]
        self.arena_t = sb("arena", [128, ARENA_WORDS])
        self.A = Arena(self.arena_t)
        self.ps = [st.enter_context(nc.psum_tensor(f"ps{i}", [128, 512], F32)) for i in range(8)]
        self.psi = 0
        self.sbank = self.ps[7]
        self.mbank = self.ps[6]
        self.sqt = [sb(f"sqt{i}", [128, 512]) for i in range(2)]
        self._stat_pending = []
        self.vsum = sb("vsum", [128, 512])
        self.prefetched = set()
        self.wplan = {}
        self.scr = {}
        for name, K, M, MW, cnt in [("w1", D, DFF, 512, 4), ("w3", D, DFF, 512, 4), ("w2", DFF, D, 128, 4),
                                    ("win", D, PROJ, 512, 1), ("wout", DIN, D, 256, 1),
                                    ("pw1", D, 2 * D, 512, 1), ("pw2", D, D, 512, 1)]:
            ns = (M + MW - 1) // MW
            self.wplan[name] = (K, M, MW, ns)
            self.scr[name] = nc.dram_tensor("scr_" + name, [cnt, ns, 128, SLOTW], BF16).ap()
        self.wplan["dgs"] = (24 * 128, 128 * 4, 128, 4)
        self.wplan["dgc"] = (KW * 128, 128 * 8, 128, 8)
        self.scr["dgs"] = nc.dram_tensor("scr_dgs", [1, 4, 128, SLOTW], BF16).ap()
        self.scr["dgc"] = nc.dram_tensor("scr_dgc", [1, 8, 128, SLOTW], BF16).ap()
        self.build_wseq()
        self.wpos = 0
        self.wloaded = 0
        self.G0 = None
        self.depth = NSLOT
        self.xslots = []

    def bank(self):
        b = self.ps[self.psi]
        self.psi = (self.psi + 1) % 6
        return b

    def vc(self, name, n=1, off=0):
        r = self.vr[name] + off
        return self.vcol[:, r:r + n]

    def build_wseq(self):
        seq = []

        def slabs(name, idx, order=None):
            K, M, MW, ns = self.wplan[name]
            for s in (order if order is not None else range(ns)):
                mw = min(MW, M - s * MW)
                seq.append((name, idx, s, K // 128, mw))

        def ffn(i, w):
            K, M, MW, ns = self.wplan["w1"]
            for s in range(ns):
                mw = min(MW, M - s * MW)
                seq.append(("w1", i * 2 + w, s, 8, mw))
                seq.append(("w3", i * 2 + w, s, 8, mw))
            slabs("w2", i * 2 + w)

        ffn(0, 0)
        slabs("win", 0, order=[4, 5, 6, 7, 8, 9, 10])
        slabs("dgs", 0)
        slabs("win", 0, order=[0, 1, 2, 3])
        slabs("wout", 0)
        ffn(0, 1)
        ffn(1, 0)
        slabs("pw1", 0)
        slabs("dgc", 0)
        slabs("pw2", 0)
        ffn(1, 1)
        self.wseq = seq

    def cast_weights(self):
        done = set()
        self._ncast = 0
        for (name, idx, s, KC, mw) in self.wseq:
            if (name, idx, s) in done or name in ("dgs", "dgc"):
                continue
            done.add((name, idx, s))
            K, M, MW, ns = self.wplan[name]
            wap = self.w[name]
            wap = wap[idx // 2, idx % 2] if name in ("w1", "w3", "w2") else wap[0]
            src = wap.rearrange("(k p) m -> p k m", p=128)[:, :, s * MW:s * MW + mw]
            dst = self.scr[name][idx, s][:, :KC * mw].rearrange("p (k m) -> p k m", k=KC)
            o = self.p.dma(POOL, dst, src, key=f"cast{self._ncast % 8}")
            if self._ncast == 8:
                for k_ in ("mw0", "mw1"):
                    o.dma_wait[k_] = max(o.dma_wait.get(k_, 0), self.p.dma_counts.get(k_, 0) - 32)
            self._ncast += 1

    def _slot_of(self, g):
        if self.G0 is None or g < self.G0 + NSLOT:
            return self.ring[g % NSLOT], f"w{g % NSLOT}"
        i = (g - self.G0 - NSLOT) % len(self.xslots)
        return self.xslots[i], f"wx{i}"

    def _load_slab(self, g):
        ntile_seq = len(self.wseq)
        if g >= self.total_slabs:
            return
        name, idx, s, KC, mw = self.wseq[g % ntile_seq]
        slot, key = self._slot_of(g)
        src = self.scr[name][idx, s][:, :KC * mw]
        dst = slot[:, :KC * mw]
        self.p.dma(SP, dst, src, key=key)

    def start_wstream(self, ntiles):
        self.total_slabs = ntiles * len(self.wseq)
        for g in range(NSLOT):
            self._load_slab(g)
        self.wloaded = NSLOT

    def enter_sample_mode(self, nextra=8):
        self.xslots = [self.A.alloc([SLOTW], BF16) for _ in range(nextra)]
        self.G0 = self.wpos
        self.depth = NSLOT + nextra
        for g in range(self.G0 + NSLOT, self.G0 + NSLOT + nextra):
            self._load_slab(g)

    def next_slab(self, expect):
        g = self.wpos
        name, idx, s, KC, mw = self.wseq[g % len(self.wseq)]
        assert name == expect, (name, expect)
        self.wpos += 1
        slot, _ = self._slot_of(g)
        v = slot[:, :KC * mw].rearrange("p (k m) -> p k m", k=KC)
        return v, mw, g

    def slab_done(self, g):
        if self.G0 is None:
            self._load_slab(g + NSLOT)
        elif g >= self.G0 + NSLOT:
            self._load_slab(g + len(self.xslots))

    def setup(self):
        p, A = self.p, self.A
        ident = self.ident
        p.dma(SP, self.cst[:], self.consts, key="c0")
        p.op(DVE, lambda e: e.tensor_copy(out=self.bdb[:], in_=self.bd), reads=[self.bd], writes=[self.bdb[:]])
        p.op(DVE, lambda e: e.tensor_copy(out=self.mskb[:, 0:64], in_=self.tri), reads=[self.tri], writes=[self.mskb[:, 0:64]])
        p.op(DVE, lambda e: e.tensor_copy(out=self.mskb[:, 64:128], in_=self.idl), reads=[self.idl], writes=[self.mskb[:, 64:128]])
        m0 = A.mark()
        nblk = self.nvr // 128
        stg = [A.alloc([128], F32) for _ in range(2)]
        for b in range(nblk):
            s = stg[b % 2]
            p.dma(SP, s, self.vecs[b * 128:(b + 1) * 128, :], key=f"vs{b % 2}")
            bk = self.bank()
            p.op(PE, lambda e, bk=bk, s=s: e.transpose(bk[:, 0:128], s, ident), reads=[s, ident], writes=[bk])
            dst = self.vcol[:, b * 128:(b + 1) * 128]
            p.op(DVE, lambda e, bk=bk, dst=dst: e.tensor_copy(out=dst, in_=bk[:, 0:128]), reads=[bk], writes=[dst])
        al = self.vc("alog")
        p.op(ACT, lambda e: e.activation(out=self.acol[0:32, :], in_=al[0:32, :], func=AF.Exp),
             reads=[al[0:32, :]], writes=[self.acol[0:32, :]])
        p.op(DVE, lambda e: e.tensor_scalar(out=self.acol[0:32, :], in0=self.acol[0:32, :], scalar1=-1.0, scalar2=None,
                                            op0=ALU.mult),
             reads=[self.acol[0:32, :]], writes=[self.acol[0:32, :]])
        c2sb = A.alloc([D], F32)
        p.dma(SP, c2sb[0:2, :], self.c2, key="c1")
        cact = A.alloc([8, 2], F32)
        bk = self.bank()
        for k in range(8):
            p.op(PE, lambda e, k=k, bk=bk: e.transpose(bk[:, 2 * k:2 * k + 2], c2sb[0:2, k * 128:(k + 1) * 128], ident[0:2, 0:2]),
                 reads=[c2sb[0:2, k * 128:(k + 1) * 128], ident[0:2, 0:2]], writes=[bk])
        p.op(ACT, lambda e, bk=bk: e.activation(out=cact, in_=bk[:, 0:16].rearrange("p (k g) -> p k g", k=8), func=AF.Silu),
             reads=[bk], writes=[cact])
        wst = [A.alloc([8, 512], F32) for _ in range(2)]
        wsb = [A.alloc([8, 512], BF16) for _ in range(2)]
        rowb = [A.alloc([512], F32) for _ in range(2)]
        cactb = A.alloc([8, 2], BF16)
        p.op(DVE, lambda e: e.tensor_copy(out=cactb, in_=cact), reads=[cact], writes=[cactb])
        n = 0
        for i in range(2):
            bkT = self.sbank
            for s in range(18):
                wt = wst[n % 2]
                wb = wsb[n % 2]
                rb = rowb[n % 2]
                p.dma(SP, wt, self.mod_w[i].rearrange("(k p) m -> p k m", p=128)[:, :, s * 512:(s + 1) * 512], key=f"mw{n % 2}")
                if n % 2 == 0:
                    p.op(DVE, lambda e, wt=wt, wb=wb: e.tensor_copy(out=wb, in_=wt), reads=[wt], writes=[wb])
                else:
                    p.op(ACT, lambda e, wt=wt, wb=wb: e.activation(out=wb, in_=wt, func=AF.Copy), reads=[wt], writes=[wb])
                n += 1
                bk = self.bank()
                for k in range(8):
                    p.op(PE, lambda e, bk=bk, wb=wb, k=k: e.matmul(bk[0:2, :], cactb[:, k, :], wb[:, k, :], start=(k == 0), stop=(k == 7)),
                         reads=[cactb[:, k, :], wb[:, k, :]], writes=[bk])
                p.op(DVE, lambda e, bk=bk, rb=rb: e.tensor_copy(out=rb[0:2, :], in_=bk[0:2, :]), reads=[bk], writes=[rb[0:2, :]])
                for jj in range(4):
                    j = 4 * s + jj
                    p.op(PE, lambda e, bkT=bkT, rb=rb, jj=jj, j=j: e.transpose(bkT[:, 2 * j:2 * j + 2], rb[0:2, jj * 128:(jj + 1) * 128], ident[0:2, 0:2]),
                         reads=[rb[0:2, jj * 128:(jj + 1) * 128], ident[0:2, 0:2]], writes=[bkT])
            mb = self.vc(f"mb{i}", 72)
            dst = self.modc[:, i]
            p.op(DVE, lambda e, bkT=bkT, dst=dst, mb=mb: e.tensor_tensor(
                out=dst, in0=bkT[:, 0:144].rearrange("p (j g) -> p j g", j=72),
                in1=bc(mb.unsqueeze(2), [128, 72, 2]), op=ALU.add), reads=[bkT, mb], writes=[dst])
        for g in range(2):
            for i in range(2):
                for s in range(3):
                    sc = self.modc[:, i, (3 * s + 1) * 8:(3 * s + 2) * 8, g]
                    ng = self.vc(f"ng{i}{s}", 8)
                    dst = self.der[:, g, i, s, :]
                    p.op(DVE, lambda e, sc=sc, ng=ng, dst=dst: e.scalar_tensor_tensor(
                        out=dst, in0=sc, scalar=1.0, in1=ng, op0=ALU.add, op1=ALU.mult), reads=[sc, ng], writes=[dst])
                for s, slot in ((0, 3), (2, 4)):
                    gt = self.modc[:, i, (3 * s + 2) * 8:(3 * s + 3) * 8, g]
                    dst = self.der[:, g, i, slot, :]
                    p.op(DVE, lambda e, gt=gt, dst=dst: e.tensor_scalar(out=dst, in0=gt, scalar1=0.5, scalar2=None, op0=ALU.mult),
                         reads=[gt], writes=[dst])
                gt2 = self.modc[:, i, 40:48, g]
                dst = self.der[:, g, i, 5, :]
                b2 = self.vc("bp2", 8)
                p.op(DVE, lambda e, gt2=gt2, dst=dst, b2=b2: e.tensor_tensor(out=dst, in0=gt2, in1=b2, op=ALU.mult),
                     reads=[gt2, b2], writes=[dst])
        dst_ = [A.alloc([SLOTW], BF16) for _ in range(2)]
        nb_ = 0
        r0 = self.vr["scw0"]
        for sl in range(4):
            d = dst_[nb_ % 2][:, 0:24 * 128].rearrange("p (j k q) -> p j k q", j=6, k=4)
            wbase = self.vcol[:, r0 + 6 * sl:r0 + 6 * sl + 1]
            wk = bass.AP(wbase.tensor, wbase.offset, [list(wbase.ap[0]), [1, 6], [24, 4], [0, 128]])
            idb = bass.AP(ident.tensor, ident.offset, [list(ident.ap[0]), [0, 6], [0, 4], [1, 128]])
            p.op(DVE, lambda e, d=d, wk=wk, idb=idb: e.tensor_tensor(out=d, in0=idb, in1=wk, op=ALU.mult),
                 reads=[ident, self.vcol[:, r0:r0 + 96]], writes=[dst_[nb_ % 2][:, 0:24 * 128]])
            p.dma(SP, self.scr["dgs"][0, sl][:, 0:24 * 128], dst_[nb_ % 2][:, 0:24 * 128], key=f"dgb{nb_ % 2}")
            nb_ += 1
        r0 = self.vr["dw0"]
        for c in range(8):
            d = dst_[nb_ % 2][:, 0:KW * 128].rearrange("p (k q) -> p k q", k=KW)
            wbase = self.vcol[:, r0 + c:r0 + c + 1]
            wk = bass.AP(wbase.tensor, wbase.offset, [list(wbase.ap[0]), [8, KW], [0, 128]])
            p.op(DVE, lambda e, d=d, wk=wk: e.tensor_tensor(out=d, in0=bc(ident.unsqueeze(1), [128, KW, 128]), in1=wk, op=ALU.mult),
                 reads=[ident, self.vcol[:, r0:r0 + 8 * KW]], writes=[dst_[nb_ % 2][:, 0:KW * 128]])
            p.dma(SP, self.scr["dgc"][0, c][:, 0:KW * 128], dst_[nb_ % 2][:, 0:KW * 128], key=f"dgb{nb_ % 2}")
            nb_ += 1
        A.release(m0)

    def gsc(self, g, i, s):
        return self.der[:, g, i, s, :]

    def shc(self, g, i, s):
        return self.modc[:, i, (3 * s) * 8:(3 * s + 1) * 8, g]

    def init_state_zero(self):
        p = self.p
        for t in (self.H, self.xl32, self.uhist):
            p.op(DVE, lambda e, t=t: e.memset(t[:], 0.0), writes=[t[:]])
        p.op(DVE, lambda e: e.memset(self.Hb[:], 0.0), writes=[self.Hb[:]])

    def hb_data(self, rows=slice(0, 128)):
        t = self.Hb
        a = t[rows, :, :, 0:64]
        dims = a.ap
        row = dims[0]
        return bass.AP(a.tensor, a.offset, [list(row), [256, 16], [192, 2], [1, 64]])

    def refresh_Hb(self, gq=None):
        p = self.p
        if gq is None:
            dst = self.hb_data()
            src = self.H[:].rearrange("p (c e) q -> p c e q", e=2)
            p.op(ACT, lambda e: e.activation(out=dst, in_=src, func=AF.Copy), reads=[self.H[:]], writes=[self.Hb[:]])
            return
        a = self.Hb[:, 4 * gq:4 * gq + 4, :, 0:64]
        dst = bass.AP(a.tensor, a.offset, [list(a.ap[0]), [256, 4], [192, 2], [1, 64]])
        hsrc = self.H[:, 8 * gq:8 * gq + 8, :]
        src = hsrc.rearrange("p (c e) q -> p c e q", e=2)
        p.op(ACT, lambda e: e.activation(out=dst, in_=src, func=AF.Copy), reads=[hsrc], writes=[self.Hb[:, 4 * gq:4 * gq + 4]])

    def load_sample_state(self):
        p, A, ident = self.p, self.A, self.ident
        m0 = A.mark()
        stg = [A.alloc([128], F32) for _ in range(2)]
        for b in range(16):
            s = stg[b % 2]
            p.dma(SP, s, self.st_in[b * 128:(b + 1) * 128, :], key=f"vs{b % 2}")
            bk = self.bank()
            p.op(PE, lambda e, bk=bk, s=s: e.transpose(bk[:, 0:128], s, ident), reads=[s, ident], writes=[bk])
            dst = self.H[:, 2 * b:2 * b + 2, :]
            p.op(DVE, lambda e, bk=bk, dst=dst: e.tensor_copy(out=dst, in_=bk[:, 0:128].rearrange("p (h q) -> p h q", h=2)),
                 reads=[bk], writes=[dst])
        self.refresh_Hb()
        cs = A.alloc([CONVD], F32)
        p.dma(SP, cs[0:3, :], self.sc_in, key="c2")
        bk = self.bank()
        for j in range(24):
            p.op(PE, lambda e, bk=bk, j=j: e.transpose(bk[:, 3 * j:3 * j + 3], cs[0:3, j * 128:(j + 1) * 128], ident[0:3, 0:3]),
                 reads=[cs[0:3, j * 128:(j + 1) * 128], ident[0:3, 0:3]], writes=[bk])
        p.op(DVE, lambda e, bk=bk: e.tensor_copy(out=self.xl32[:], in_=bk[:, 0:72].rearrange("p (j t) -> p j t", j=24)),
             reads=[bk], writes=[self.xl32[:]])
        cc = A.alloc([D], F32)
        p.dma(SP, cc[0:KW - 1, :], self.cc_in, key="c3")
        bk = self.bank()
        for c in range(8):
            p.op(PE, lambda e, bk=bk, c=c: e.transpose(bk[:, 30 * c:30 * c + 30], cc[0:30, c * 128:(c + 1) * 128], ident[0:30, 0:30]),
                 reads=[cc[0:30, c * 128:(c + 1) * 128], ident[0:30, 0:30]], writes=[bk])
        p.op(DVE, lambda e, bk=bk: e.tensor_copy(out=self.uhist[:], in_=bk[:, 0:240].rearrange("p (c t) -> p c t", c=8)),
             reads=[bk], writes=[self.uhist[:]])
        A.release(m0)

    def store_states(self, g):
        p, A, ident = self.p, self.A, self.ident
        m0 = A.mark()
        stg = [A.alloc([128], F32) for _ in range(2)]
        for b in range(16):
            s = stg[b % 2]
            bk = self.bank()
            src = self.H[:, 2 * b:2 * b + 2, :]
            p.op(PE, lambda e, bk=bk, src=src: e.transpose(bk[:, 0:128], src.rearrange("p h q -> p (h q)"), ident),
                 reads=[src, ident], writes=[bk])
            p.op(DVE, lambda e, bk=bk, s=s: e.tensor_copy(out=s, in_=bk[:, 0:128]), reads=[bk], writes=[s])
            p.dma(POOL, self.o_st[g][b * 128:(b + 1) * 128, :], s, key=f"so{b % 2}")
        cs = A.alloc([CONVD], F32)
        for q in range(6):
            bk = self.bank()
            for jj in range(4):
                j = 4 * q + jj
                src = self.xl32[:, j, :]
                p.op(PE, lambda e, bk=bk, jj=jj, src=src: e.transpose(bk[0:3, jj * 128:(jj + 1) * 128], src, ident),
                     reads=[src, ident], writes=[bk])
            p.op(DVE, lambda e, bk=bk, q=q: e.tensor_copy(out=cs[0:3, q * 512:(q + 1) * 512], in_=bk[0:3, :]),
                 reads=[bk], writes=[cs[0:3, q * 512:(q + 1) * 512]])
        p.dma(POOL, self.o_sc[g], cs[0:3, :], key="so2")
        cc = A.alloc([D], F32)
        for q in range(2):
            bk = self.bank()
            for cq in range(4):
                c = 4 * q + cq
                src = self.uhist[:, c, :]
                p.op(PE, lambda e, bk=bk, cq=cq, src=src: e.transpose(bk[0:30, cq * 128:(cq + 1) * 128], src, ident),
                     reads=[src, ident], writes=[bk])
            p.op(DVE, lambda e, bk=bk, q=q: e.tensor_copy(out=cc[0:30, q * 512:(q + 1) * 512], in_=bk[0:30, :]),
                 reads=[bk], writes=[cc[0:30, q * 512:(q + 1) * 512]])
        p.dma(POOL, self.o_cc[g], cc[0:30, :], key="so3")
        self._pending_store_bufs = [stg[0], stg[1], cs[0:3, :], cc[0:30, :]]
        A.release(m0)

    def prefetch_x(self, src, r0, TT):
        nb = (TT + 127) // 128
        for b in range(min(2, nb)):
            rows = min(128, TT - 128 * b)
            self.p.dma(ACT, self.stg_in[b % 2][0:rows, :], src[r0 + 128 * b:r0 + 128 * b + rows, :], key=f"xi{b % 2}")
            self.prefetched.add((src.tensor.name, r0, b))

    def load_x(self, src, r0, TT):
        p, ident = self.p, self.ident
        nb = (TT + 127) // 128
        for b in range(nb):
            rows = min(128, TT - 128 * b)
            s = self.stg_in[b % 2]
            if (src.tensor.name, r0, b) not in self.prefetched:
                p.dma(ACT, s[0:rows, :], src[r0 + 128 * b:r0 + 128 * b + rows, :], key=f"xi{b % 2}")
            for half in range(2):
                bk = self.bank()
                for i in range(4):
                    c = 4 * half + i
                    p.op(PE, lambda e, bk=bk, i=i, c=c, s=s, rows=rows: e.transpose(
                        bk[:, i * rows:(i + 1) * rows], s[0:rows, c * 128:(c + 1) * 128], ident[0:rows, 0:rows]),
                        reads=[s[0:rows, c * 128:(c + 1) * 128], ident[0:rows, 0:rows]], writes=[bk])
                dst = self.xres[:, 4 * half:4 * half + 4, 128 * b:128 * b + rows]
                p.op(ACT, lambda e, bk=bk, dst=dst, rows=rows: e.activation(
                    out=dst, in_=bk[:, 0:4 * rows].rearrange("p (c t) -> p c t", c=4), func=AF.Copy),
                    reads=[bk], writes=[dst])
        for m in range(8):
            self.stat_push(TT, m)

    def store_y(self, yT, dst, r0, TT):
        p, ident, A = self.p, self.ident, self.A
        nb = (TT + 127) // 128
        so = [A.alloc([D], F32) for _ in range(2)]
        for b in range(nb):
            rows = min(128, TT - 128 * b)
            s = so[b % 2]
            for half in range(2):
                bk = self.bank()
                for i in range(4):
                    c = 4 * half + i
                    src = yT[:, c, 128 * b:128 * b + rows]
                    p.op(PE, lambda e, bk=bk, i=i, src=src, rows=rows: e.transpose(
                        bk[0:rows, i * 128:(i + 1) * 128], src, ident), reads=[src, ident], writes=[bk])
                d2 = s[0:rows, half * 512:(half + 1) * 512]
                p.op(ACT, lambda e, bk=bk, d2=d2, rows=rows: e.activation(out=d2, in_=bk[0:rows, :], func=AF.Copy),
                     reads=[bk], writes=[d2])
            p.dma(ACT, dst[r0 + 128 * b:r0 + 128 * b + rows, :], s[0:rows, :], key=f"yo{b % 2}")

    def acc_new(self, bank, nch, TT, square):
        return {"bank": bank, "nch": nch, "TT": TT, "square": square, "pending": []}

    def acc_push(self, acc, chunk, m):
        p, TT, nch, bank = self.p, acc["TT"], acc["nch"], acc["bank"]
        ssum = (self.sqt[1] if acc["square"] else self.vsum)[:, 0:TT]
        if acc["square"]:
            sq = self.sqt[0][:, 0:TT]
            p.op(ACT, lambda e: e.activation(out=sq, in_=chunk, func=AF.Square), reads=[chunk], writes=[sq])
            src = sq
        else:
            src = chunk
        if m == 0:
            p.op(DVE, lambda e: e.tensor_copy(out=ssum, in_=src), reads=[src], writes=[ssum])
        else:
            p.op(DVE, lambda e: e.tensor_tensor(out=ssum, in0=ssum, in1=src, op=ALU.add), reads=[ssum, src], writes=[ssum])
        if m == nch - 1:
            p.op(PE, lambda e: e.matmul(bank[:, 0:TT], self.ones, ssum, start=True, stop=True), reads=[self.ones, ssum], writes=[bank])

    def stat_push(self, TT, m, nch=8):
        if m == 0:
            self._racc = self.acc_new(self.sbank, nch, TT, True)
        self.acc_push(self._racc, self.xres[:, m, 0:TT], m)

    def rms(self, src, nch, TT, scale_cols, bias_cols, out, func=None, pre_stats=False, mid=None, halves=False):
        p, A = self.p, self.A
        m0 = A.mark()
        tmp = A.alloc([nch, TT], F32)
        rstd = A.alloc([TT], F32)
        hh = max(1, nch // 2)
        if pre_stats:
            bk = self.sbank
        else:
            for h0 in range(0, nch, hh):
                p.op(ACT, lambda e, h0=h0: e.activation(out=tmp[:, h0:h0 + hh, :], in_=src[:, h0:h0 + hh, :], func=AF.Square),
                     reads=[src[:, h0:h0 + hh, :]], writes=[tmp[:, h0:h0 + hh, :]])
            bk = self.bank()
            for c in range(nch):
                p.op(PE, lambda e, bk=bk, c=c: e.matmul(bk[:, 0:TT], self.ones, tmp[:, c, :], start=(c == 0), stop=(c == nch - 1)),
                     reads=[self.ones, tmp[:, c, :]], writes=[bk])
        p.op(ACT, lambda e, bk=bk: e.activation(out=rstd, in_=bk[:, 0:TT], func=AF.Sqrt, bias=EPS, scale=1.0 / (128 * nch)),
             reads=[bk], writes=[rstd])
        p.op(DVE, lambda e: e.reciprocal(out=rstd, in_=rstd), reads=[rstd], writes=[rstd])
        q4 = max(1, nch // 4)
        spans = [(0, TT)] if not (halves and TT == 512) else [(0, 256), (256, 512)]
        for si, (ta, tb) in enumerate(spans):
            tw = tb - ta
            step = q4 if len(spans) == 1 else max(1, nch // 2)
            for h0 in range(0, nch, step):
                p.op(DVE, lambda e, h0=h0, ta=ta, tb=tb, tw=tw, step=step: e.tensor_tensor(
                    out=tmp[:, h0:h0 + step, ta:tb], in0=src[:, h0:h0 + step, ta:tb],
                    in1=bc(rstd[:, ta:tb].unsqueeze(1), [128, step, tw]), op=ALU.mult),
                    reads=[src[:, h0:h0 + step, ta:tb], rstd[:, ta:tb]], writes=[tmp[:, h0:h0 + step, ta:tb]])
            if mid is not None and si == len(spans) - 1:
                mid()
            for c in range(nch):
                sc = scale_cols[:, c:c + 1]
                bi = bias_cols[:, c:c + 1] if bias_cols is not None else 0.0
                rd = [tmp[:, c, ta:tb], sc] + ([bi] if bias_cols is not None else [])
                p.op(ACT, lambda e, c=c, sc=sc, bi=bi, ta=ta, tb=tb: e.activation(out=out[:, c, ta:tb], in_=tmp[:, c, ta:tb],
                                                                                func=(func or AF.Identity), bias=bi, scale=sc),
                     reads=rd, writes=[out[:, c, ta:tb]])
        A.release(m0)

    def ffn(self, TT, g, i, s, w):
        p, A = self.p, self.A
        xres = self.xres[:, :, 0:TT]
        m0 = A.mark()
        hbf = A.alloc([8, TT], BF16)
        hid = A.alloc([22, TT], BF16)
        sil = [A.alloc([TT], F32) for _ in range(2)]
        self.rms(xres, 8, TT, self.gsc(g, i, s), self.shc(g, i, s), hbf, pre_stats=True, halves=True)
        gate = self.der[:, g, i, 3 if s == 0 else 4, :]
        j = 0
        for sl in range(6):
            w1s, mw, g1 = self.next_slab("w1")
            w3s, _, g3 = self.next_slab("w3")
            def evac(pa, pb, j):
                st = sil[j % 2]
                p.op(ACT, lambda e, pa=pa, st=st: e.activation(out=st, in_=pa[:, 0:TT], func=AF.Silu), reads=[pa], writes=[st])
                p.op(DVE, lambda e, pb=pb, st=st, j=j: e.tensor_tensor(out=hid[:, j, :], in0=st, in1=pb[:, 0:TT], op=ALU.mult),
                     reads=[st, pb], writes=[hid[:, j, :]])

            jj0 = 0
            if sl == 0 and TT == 512:
                bks = [(self.bank(), self.bank()) for _ in range(2)]
                for (ta, tb) in ((0, 256), (256, 512)):
                    for jj in range(2):
                        for bkx, wsx in ((bks[jj][0], w1s), (bks[jj][1], w3s)):
                            for k in range(8):
                                p.op(PE, lambda e, bkx=bkx, wsx=wsx, k=k, jj=jj, ta=ta, tb=tb: e.matmul(
                                    bkx[:, ta:tb], wsx[:, k, jj * 128:(jj + 1) * 128], hbf[:, k, ta:tb], start=(k == 0), stop=(k == 7)),
                                    reads=[wsx[:, k, jj * 128:(jj + 1) * 128], hbf[:, k, ta:tb]], writes=[bkx])
                for jj in range(2):
                    evac(bks[jj][0], bks[jj][1], j)
                    j += 1
                jj0 = 2
            for jj in range(jj0, mw // 128):
                pa = self.bank()
                for k in range(8):
                    p.op(PE, lambda e, pa=pa, k=k, jj=jj, w1s=w1s: e.matmul(
                        pa[:, 0:TT], w1s[:, k, jj * 128:(jj + 1) * 128], hbf[:, k, :], start=(k == 0), stop=(k == 7)),
                        reads=[w1s[:, k, jj * 128:(jj + 1) * 128], hbf[:, k, :]], writes=[pa])
                pb = self.bank()
                for k in range(8):
                    p.op(PE, lambda e, pb=pb, k=k, jj=jj, w3s=w3s: e.matmul(
                        pb[:, 0:TT], w3s[:, k, jj * 128:(jj + 1) * 128], hbf[:, k, :], start=(k == 0), stop=(k == 7)),
                        reads=[w3s[:, k, jj * 128:(jj + 1) * 128], hbf[:, k, :]], writes=[pb])
                evac(pa, pb, j)
                j += 1
            self.slab_done(g1)
            self.slab_done(g3)
        for m in range(8):
            w2s, _, g2 = self.next_slab("w2")
            pc = self.bank()
            for k in range(22):
                p.op(PE, lambda e, pc=pc, k=k, w2s=w2s: e.matmul(pc[:, 0:TT], w2s[:, k, :], hid[:, k, :], start=(k == 0), stop=(k == 21)),
                     reads=[w2s[:, k, :], hid[:, k, :]], writes=[pc])
            self.slab_done(g2)
            xm = xres[:, m, :]
            p.op(DVE, lambda e, pc=pc, xm=xm, m=m: e.scalar_tensor_tensor(
                out=xm, in0=pc[:, 0:TT], scalar=gate[:, m:m + 1], in1=xm, op0=ALU.mult, op1=ALU.add),
                reads=[pc, gate[:, m:m + 1], xm], writes=[xm])
            self.stat_push(TT, m)
        A.release(m0)

    def ssd(self, TT, g):
        p, A, ident, ones, tri, bd = self.p, self.A, self.ident, self.ones, self.tri, self.bd
        L = min(64, TT)
        GT = min(128, TT)
        nchk = GT // L
        ngrp = TT // GT
        R = GT
        xres = self.xres[:, :, 0:TT]
        H, Hb = self.H, self.Hb
        m0 = A.mark()
        hbf = A.alloc([8, TT], BF16)
        self.rms(xres, 8, TT, self.gsc(g, 0, 1), self.shc(g, 0, 1), hbf, pre_stats=True)
        xs32 = A.alloc([16, TT], F32)
        BT = A.alloc([4, TT], F32)
        CT = A.alloc([4, TT], F32)
        dtT = A.alloc([TT], F32)
        dtAT = A.alloc([TT], F32)
        m1 = A.mark()
        xbc = A.alloc([24, TT + 3], BF16)
        tv = [A.alloc([TT], F32) for _ in range(3)]
        p.op(DVE, lambda e: e.tensor_copy(out=xbc[:, :, 0:3], in_=self.xl32[:]), reads=[self.xl32[:]], writes=[xbc[:, :, 0:3]])
        for sl in range(4, 10):
            ws, mw, gi = self.next_slab("win")
            for jj in range(4):
                j = (sl - 4) * 4 + jj
                ps = self.bank()
                for k in range(8):
                    p.op(PE, lambda e, ps=ps, k=k, jj=jj, ws=ws: e.matmul(
                        ps[:, 0:TT], ws[:, k, jj * 128:(jj + 1) * 128], hbf[:, k, :], start=(k == 0), stop=(k == 7)),
                        reads=[ws[:, k, jj * 128:(jj + 1) * 128], hbf[:, k, :]], writes=[ps])
                p.op(ACT, lambda e, ps=ps, j=j: e.activation(out=xbc[:, j, 3:3 + TT], in_=ps[:, 0:TT], func=AF.Copy),
                     reads=[ps], writes=[xbc[:, j, 3:3 + TT]])
                p.op(DVE, lambda e, ps=ps, j=j: e.tensor_copy(out=self.xl32[:, j, :], in_=ps[:, TT - 3:TT]),
                     reads=[ps], writes=[self.xl32[:, j, :]])
            self.slab_done(gi)
        ws, mw, gi = self.next_slab("win")
        assert mw == 32
        ps = self.bank()
        for k in range(8):
            p.op(PE, lambda e, ps=ps, k=k, ws=ws: e.matmul(ps[0:32, 0:TT], ws[:, k, 0:32], hbf[:, k, :], start=(k == 0), stop=(k == 7)),
                 reads=[ws[:, k, 0:32], hbf[:, k, :]], writes=[ps])
        self.slab_done(gi)
        v, av, ll = (t[0:32, :] for t in tv)
        dtb = self.vc("dtb")[0:32, :]
        p.op(DVE, lambda e, ps=ps: e.tensor_scalar(out=v, in0=ps[0:32, 0:TT], scalar1=dtb, scalar2=None, op0=ALU.add),
             reads=[ps, dtb], writes=[v])
        p.op(ACT, lambda e: e.activation(out=av, in_=v, func=AF.Abs), reads=[v], writes=[av])
        p.op(ACT, lambda e: e.activation(out=av, in_=av, func=AF.Exp, scale=-1.0), reads=[av], writes=[av])
        p.op(ACT, lambda e: e.activation(out=ll, in_=av, func=AF.Ln, bias=1.0), reads=[av], writes=[ll])
        p.op(DVE, lambda e: e.scalar_tensor_tensor(out=dtT[0:32, :], in0=v, scalar=0.0, in1=ll, op0=ALU.max, op1=ALU.add),
             reads=[v, ll], writes=[dtT[0:32, :]])
        p.op(DVE, lambda e: e.tensor_scalar(out=dtAT[0:32, :], in0=dtT[0:32, :], scalar1=self.acol[0:32, :], scalar2=None, op0=ALU.mult),
             reads=[dtT[0:32, :], self.acol[0:32, :]], writes=[dtAT[0:32, :]])
        for sl in range(4):
            ws, mw, gi = self.next_slab("dgs")
            for jj in range(6):
                j = 6 * sl + jj
                ps = self.bank()
                for k in range(4):
                    p.op(PE, lambda e, ps=ps, k=k, ws=ws, jj=jj, j=j: e.matmul(ps[:, 0:TT], ws[:, 4 * jj + k, :], xbc[:, j, k:k + TT],
                                                                        start=(k == 0), stop=(k == 3)),
                         reads=[ws[:, 4 * jj + k, :], xbc[:, j, k:k + TT]], writes=[ps])
                dst = xs32[:, j, :] if j < 16 else (BT[:, j - 16, :] if j < 20 else CT[:, j - 20, :])
                cb = self.vc("scb", 24)[:, j:j + 1]
                p.op(ACT, lambda e, ps=ps, dst=dst, cb=cb: e.activation(out=dst, in_=ps[:, 0:TT], func=AF.Silu, bias=cb),
                     reads=[ps, cb], writes=[dst])
            self.slab_done(gi)
        A.release(m1)
        dtokq = [A.alloc([64], F32) for _ in range(2)]
        dhl = A.alloc([64], BF16)
        rhs1h = A.alloc([16, L], BF16)
        rhs1l = A.alloc([16, L], BF16)
        decay = A.alloc([16, L], F32)
        scm = A.alloc([4, L], F32)
        wts2 = [A.alloc([32, L], BF16) for _ in range(2)]
        xdtp2 = [A.alloc([16, 2, 128], BF16) for _ in range(2)]
        xend2 = [A.alloc([32, 64], BF16) for _ in range(2)]
        btok2 = [A.alloc([4, 128], BF16) for _ in range(2)]
        ctok2 = [A.alloc([4, 128], BF16) for _ in range(2)]
        eeq = [A.alloc([64], F32) for _ in range(2)]
        eal2 = [A.alloc([nchk, 32], F32) for _ in range(2)]
        dgm2 = [A.alloc([32, L], BF16) for _ in range(2)]
        cexp = A.alloc([32, L], BF16)
        dcol = self.vc("dexp", 16)
        bdb, bdi, idl = self.bdb, self.bdi, self.idl
        trib, idlb = self.mskb[:, 0:64], self.mskb[:, 64:128]
        eab = [A.alloc([32], BF16) for _ in range(2)]
        for xd_ in xdtp2:
            p.op(DVE, lambda e, xd_=xd_: e.memset(xd_, 0.0), writes=[xd_])

        def xdtp_data(xdtp, rows, c0, ncx):
            a = xdtp[rows, c0:c0 + ncx, :, 0:64]
            return bass.AP(a.tensor, a.offset, [list(a.ap[0]), [256, ncx], [192, 2], [1, 64]])

        def bufs(q):
            i = q % 2
            return wts2[i], xdtp2[i], xend2[i], btok2[i], ctok2[i], eal2[i], dgm2[i]

        def prep_a(q):
            t0 = q * GT
            wts, xdtp, xend, btok, ctok, eal, dgm = bufs(q)
            dtok = dtokq[q % 2]
            ee = eeq[q % 2]
            bkt = self.bank()
            p.op(PE, lambda e, bkt=bkt, t0=t0: e.transpose(bkt[0:R, 0:32], dtT[0:32, t0:t0 + R], ident[0:32, 0:32]),
                 reads=[dtT[0:32, t0:t0 + R], ident[0:32, 0:32]], writes=[bkt])
            p.op(PE, lambda e, bkt=bkt, t0=t0: e.transpose(bkt[0:R, 32:64], dtAT[0:32, t0:t0 + R], ident[0:32, 0:32]),
                 reads=[dtAT[0:32, t0:t0 + R], ident[0:32, 0:32]], writes=[bkt])
            p.op(DVE, lambda e, bkt=bkt: e.tensor_copy(out=dtok[0:R, :], in_=bkt[0:R, 0:64]), reads=[bkt], writes=[dtok[0:R, :]])
            dta = dtok[0:R, 32:64]
            dhi, dlo = dhl[0:R, 0:32], dhl[0:R, 32:64]
            p.op(DVE, lambda e: e.tensor_copy(out=dhi, in_=dta), reads=[dta], writes=[dhi])
            p.op(DVE, lambda e: e.tensor_tensor(out=dlo, in0=dta, in1=dhi, op=ALU.subtract), reads=[dta, dhi], writes=[dlo])
            bke = self.bank()
            p.op(PE, lambda e, bke=bke: e.matmul(bke[0:R, 0:32], bd[0:R, 0:R], dta, start=True, stop=True),
                 reads=[bd[0:R, 0:R], dta], writes=[bke])
            p.op(PE, lambda e, bke=bke: e.matmul(bke[0:R, 32:64], bdi[0:R, 0:R], dta, start=True, stop=True),
                 reads=[bdi[0:R, 0:R], dta], writes=[bke])
            p.op(ACT, lambda e, bke=bke: e.activation(out=ee[0:R, :], in_=bke[0:R, 0:64], func=AF.Exp), reads=[bke], writes=[ee[0:R, :]])
            eend = ee[:, 0:32]
            eact = ee[:, 32:64]
            for j in range(nchk):
                rj = slice(64 * j, 64 * j + L)
                bkj = self.bank()
                p.op(PE, lambda e, bkj=bkj, rj=rj: e.matmul(bkj[:, 0:32], ones[rj, :], dtok[rj, 32:64], start=True, stop=True),
                     reads=[ones[rj, :], dtok[rj, 32:64]], writes=[bkj])
                p.op(ACT, lambda e, bkj=bkj, j=j: e.activation(out=eal[:, j, :], in_=bkj[:, 0:32], func=AF.Exp),
                     reads=[bkj], writes=[eal[:, j, :]])
            eb = eab[q % 2]
            p.op(DVE, lambda e: e.tensor_copy(out=eb[0:R, :], in_=eact[0:R, :]), reads=[eact[0:R, :]], writes=[eb[0:R, :]])
            p.op(DVE, lambda e: e.tensor_tensor(out=dgm[0:R], in0=bc(eb[0:R, :].unsqueeze(2), [R, 32, L]),
                                                in1=bc(idlb[0:R, 0:L].unsqueeze(1), [R, 32, L]), op=ALU.mult),
                 reads=[eb[0:R, :], idlb[0:R, 0:L]], writes=[dgm[0:R]])
            bkc = self.bank()
            for gq in range(4):
                p.op(PE, lambda e, bkc=bkc, gq=gq, t0=t0: e.matmul(bkc[0:R, gq * R:(gq + 1) * R], BT[:, gq, t0:t0 + R], CT[:, gq, t0:t0 + R],
                                                                  start=True, stop=True),
                     reads=[BT[:, gq, t0:t0 + R], CT[:, gq, t0:t0 + R]], writes=[bkc])
            for j in range(nchk):
                rj = slice(64 * j, 64 * j + L)
                p.op(DVE, lambda e, bkc=bkc, j=j, rj=rj: e.tensor_tensor(
                    out=scm[rj], in0=bkc[rj, 0:4 * R].rearrange("p (g l) -> p g l", g=4)[:, :, 64 * j:64 * j + L],
                    in1=bc(tri[rj, 0:L].unsqueeze(1), [L, 4, L]), op=ALU.mult), reads=[bkc, tri[rj, 0:L]], writes=[scm[rj]])
            for hh in range(2):
                for dsrc, rdst in ((dhi, rhs1h), (dlo, rhs1l)):
                    dsl = dsrc[:, 16 * hh:16 * hh + 16]
                    p.op(DVE, lambda e, dsl=dsl, rdst=rdst: e.tensor_tensor(out=rdst[0:R], in0=bc(dsl.unsqueeze(2), [R, 16, L]),
                                                                          in1=bc(trib[0:R, 0:L].unsqueeze(1), [R, 16, L]), op=ALU.mult),
                         reads=[dsl, trib[0:R, 0:L]], writes=[rdst[0:R]])
                for i2 in range(2):
                    bks = self.bank()
                    h0 = 8 * i2
                    for ri, rr in enumerate((rhs1h, rhs1l)):
                        p.op(PE, lambda e, bks=bks, h0=h0, rr=rr, ri=ri: e.matmul(bks[0:R, 0:8 * L], bdb[0:R, 0:R],
                                                                               rr[0:R, h0:h0 + 8, :].rearrange("p h l -> p (h l)"),
                                                                               start=(ri == 0), stop=(ri == 1)),
                             reads=[bdb[0:R, 0:R], rr[0:R, h0:h0 + 8, :]], writes=[bks])
                    p.op(ACT, lambda e, bks=bks, i2=i2: e.activation(out=decay[0:R, 8 * i2:8 * i2 + 8, :],
                                                                     in_=bks[0:R, 0:8 * L].rearrange("p (h l) -> p h l", h=8), func=AF.Exp),
                         reads=[bks], writes=[decay[0:R, 8 * i2:8 * i2 + 8, :]])
                p.op(DVE, lambda e, hh=hh: e.tensor_tensor(
                    out=wts[0:R, 16 * hh:16 * hh + 16, :].rearrange("p (g h) l -> p g h l", g=2),
                    in0=decay[0:R].rearrange("p (g h) l -> p g h l", g=2),
                    in1=bc(scm[0:R, 2 * hh:2 * hh + 2, :].unsqueeze(2), [R, 2, 8, L]), op=ALU.mult),
                    reads=[decay[0:R], scm[0:R, 2 * hh:2 * hh + 2, :]], writes=[wts[0:R, 16 * hh:16 * hh + 16, :]])

        def prep_b(q):
            t0 = q * GT
            wts, xdtp, xend, btok, ctok, eal, dgm = bufs(q)
            eend = eeq[q % 2][:, 0:32]
            dtok = dtokq[q % 2]
            for b4 in range(4):
                bkx = self.bank()
                for i in range(4):
                    c = 4 * b4 + i
                    src = xs32[:, c, t0:t0 + R]
                    p.op(PE, lambda e, bkx=bkx, i=i, src=src: e.transpose(bkx[0:R, i * 128:(i + 1) * 128], src, ident),
                         reads=[src, ident], writes=[bkx])
                dstv = xdtp_data(xdtp, slice(0, R), 4 * b4, 4)
                p.op(DVE, lambda e, bkx=bkx, dstv=dstv, b4=b4: e.tensor_tensor(
                    out=dstv, in0=bkx[0:R, :].rearrange("p (c e q) -> p c e q", c=4, e=2),
                    in1=bc(dtok[0:R, 8 * b4:8 * b4 + 8].rearrange("p (c e) -> p c e", c=4).unsqueeze(3), [R, 4, 2, 64]), op=ALU.mult),
                    reads=[bkx, dtok[0:R, 8 * b4:8 * b4 + 8]], writes=[xdtp[0:R, 4 * b4:4 * b4 + 4]])
            p.op(DVE, lambda e: e.tensor_tensor(
                out=xend[0:R].rearrange("p (c e) q -> p c e q", e=2), in0=xdtp_data(xdtp, slice(0, R), 0, 16),
                in1=bc(eend[0:R, :].rearrange("p (c e) -> p c e", e=2).unsqueeze(3), [R, 16, 2, 64]), op=ALU.mult),
                reads=[xdtp[0:R], eend[0:R, :]], writes=[xend[0:R]])
            xg = xs32[:, :, t0:t0 + R]
            p.op(DVE, lambda e, xg=xg: e.tensor_tensor(out=xg, in0=xg, in1=bc(dcol.unsqueeze(2), [128, 16, R]), op=ALU.mult),
                 reads=[xg, dcol], writes=[xg])
            for srcT, dtk in ((BT, btok), (CT, ctok)):
                bkb = self.bank()
                for gq in range(4):
                    src = srcT[:, gq, t0:t0 + R]
                    p.op(PE, lambda e, bkb=bkb, gq=gq, src=src: e.transpose(bkb[0:R, gq * 128:(gq + 1) * 128], src, ident),
                         reads=[src, ident], writes=[bkb])
                p.op(ACT, lambda e, bkb=bkb, dtk=dtk: e.activation(out=dtk[0:R], in_=bkb[0:R, :].rearrange("p (g n) -> p g n", g=4), func=AF.Copy),
                     reads=[bkb], writes=[dtk[0:R]])
        def chunk(q, j):
            t0 = q * GT
            wts, xdtp, xend, btok, ctok, eal, dgm = bufs(q)
            rj = slice(64 * j, 64 * j + L)
            tj = t0 + 64 * j
            for gq in range(4):
                bka = self.bank()
                p.op(PE, lambda e, bka=bka, gq=gq, rj=rj: e.matmul(bka[:, 0:8 * L], ctok[rj, gq, :],
                                                                  dgm[rj, 8 * gq:8 * gq + 8, :].rearrange("p h l -> p (h l)"), start=True, stop=True),
                     reads=[ctok[rj, gq, :], dgm[rj, 8 * gq:8 * gq + 8, :]], writes=[bka])
                p.op(ACT, lambda e, bka=bka, gq=gq: e.activation(out=cexp[:, 8 * gq:8 * gq + 8, :],
                                                                 in_=bka[:, 0:8 * L].rearrange("p (h l) -> p h l", h=8), func=AF.Copy),
                     reads=[bka], writes=[cexp[:, 8 * gq:8 * gq + 8, :]])
            for hh in range(2):
                bky = self.bank()
                for cc in range(8):
                    c = 8 * hh + cc
                    reg = bky[:, cc * L:(cc + 1) * L]
                    for e2 in range(2):
                        p.op(PE, lambda e, reg=reg, c=c, e2=e2, rj=rj: e.matmul(reg, xdtp[rj, c, e2, :], wts[rj, 2 * c + e2, :],
                                                                               start=(e2 == 0), stop=False),
                             reads=[xdtp[rj, c, e2, :], wts[rj, 2 * c + e2, :]], writes=[bky])
                    for e2 in range(2):
                        p.op(PE, lambda e, reg=reg, c=c, e2=e2: e.matmul(reg, Hb[:, c, e2, :], cexp[:, 2 * c + e2, :],
                                                                        start=False, stop=(e2 == 1)),
                             reads=[Hb[:, c, e2, :], cexp[:, 2 * c + e2, :]], writes=[bky])
                yv = xs32[:, 8 * hh:8 * hh + 8, tj:tj + L]
                p.op(DVE, lambda e, bky=bky, yv=yv: e.tensor_tensor(out=yv, in0=bky[:, 0:8 * L].rearrange("p (c l) -> p c l", c=8),
                                                                    in1=yv, op=ALU.add), reads=[bky, yv], writes=[yv])
            for gq in range(4):
                hv = H[:, 8 * gq:8 * gq + 8, :]
                p.op(DVE, lambda e, hv=hv, gq=gq, j=j: e.tensor_tensor(out=hv, in0=hv, in1=bc(eal[:, j, 8 * gq:8 * gq + 8].unsqueeze(2), [128, 8, HP]),
                                                                      op=ALU.mult), reads=[hv, eal[:, j, 8 * gq:8 * gq + 8]], writes=[hv])
                bkh = self.bank()
                p.op(PE, lambda e, bkh=bkh, gq=gq, rj=rj: e.matmul(bkh[:, 0:512], btok[rj, gq, :],
                                                                  xend[rj, 8 * gq:8 * gq + 8, :].rearrange("p h q -> p (h q)"), start=True, stop=True),
                     reads=[btok[rj, gq, :], xend[rj, 8 * gq:8 * gq + 8, :]], writes=[bkh])
                p.op(DVE, lambda e, bkh=bkh, hv=hv: e.tensor_tensor(out=hv, in0=bkh[:, 0:512].rearrange("p (h q) -> p h q", h=8),
                                                                    in1=hv, op=ALU.add), reads=[bkh, hv], writes=[hv])
                self.refresh_Hb(gq)

        prep_a(0)
        prep_b(0)
        for q in range(ngrp):
            for j in range(nchk):
                if q + 1 < ngrp:
                    if j == 0:
                        prep_a(q + 1)
                    if j == nchk - 1:
                        prep_b(q + 1)
                chunk(q, j)
        A.release(m1)
        szt = [A.alloc([TT], F32) for _ in range(2)]
        gacc = self.acc_new(self.sbank, 16, TT, True)
        for sl in range(4):
            ws, mw, gi = self.next_slab("win")
            for jj in range(4):
                j = sl * 4 + jj
                ps = self.bank()
                for k in range(8):
                    p.op(PE, lambda e, ps=ps, k=k, jj=jj, ws=ws: e.matmul(
                        ps[:, 0:TT], ws[:, k, jj * 128:(jj + 1) * 128], hbf[:, k, :], start=(k == 0), stop=(k == 7)),
                        reads=[ws[:, k, jj * 128:(jj + 1) * 128], hbf[:, k, :]], writes=[ps])
                sz = szt[j % 2]
                p.op(ACT, lambda e, ps=ps, sz=sz: e.activation(out=sz, in_=ps[:, 0:TT], func=AF.Silu), reads=[ps], writes=[sz])
                yj = xs32[:, j, :]
                p.op(DVE, lambda e, sz=sz, yj=yj: e.tensor_tensor(out=yj, in0=yj, in1=sz, op=ALU.mult), reads=[yj, sz], writes=[yj])
                self.acc_push(gacc, yj, j)
            self.slab_done(gi)
        ybf = A.alloc([16, TT], BF16)
        self.rms(xs32, 16, TT, self.vc("sng", 16), None, ybf, pre_stats=True)
        gt2 = self.modc[:, 0, 40:48, g]
        for sl in range(4):
            ws, mw, gi = self.next_slab("wout")
            for mm in range(2):
                m = 2 * sl + mm
                ps = self.bank()
                for k in range(16):
                    p.op(PE, lambda e, ps=ps, k=k, mm=mm, ws=ws: e.matmul(
                        ps[:, 0:TT], ws[:, k, mm * 128:(mm + 1) * 128], ybf[:, k, :], start=(k == 0), stop=(k == 15)),
                        reads=[ws[:, k, mm * 128:(mm + 1) * 128], ybf[:, k, :]], writes=[ps])
                xm = xres[:, m, :]
                p.op(DVE, lambda e, ps=ps, xm=xm, m=m: e.scalar_tensor_tensor(
                    out=xm, in0=ps[:, 0:TT], scalar=gt2[:, m:m + 1], in1=xm, op0=ALU.mult, op1=ALU.add),
                    reads=[ps, gt2[:, m:m + 1], xm], writes=[xm])
                self.stat_push(TT, m)
            self.slab_done(gi)
        A.release(m0)

    def convmod(self, TT, g):
        p, A, ident, ones = self.p, self.A, self.ident, self.ones
        xres = self.xres[:, :, 0:TT]
        HK = KW - 1
        m0 = A.mark()
        hbf = A.alloc([8, TT], BF16)
        self.rms(xres, 8, TT, self.gsc(g, 1, 1), self.shc(g, 1, 1), hbf, pre_stats=True)
        u32 = A.alloc([8, TT + HK], F32)
        ub = A.alloc([8, TT + HK], BF16)
        sg = [A.alloc([TT], F32) for _ in range(2)]
        p.op(DVE, lambda e: e.tensor_copy(out=u32[:, :, 0:HK], in_=self.uhist[:]), reads=[self.uhist[:]], writes=[u32[:, :, 0:HK]])
        bp1 = self.vc("bp1", 16)
        for sl in range(4):
            ws, mw, gi = self.next_slab("pw1")
            for jj in range(4):
                j = sl * 4 + jj
                c = j % 8
                ps = self.bank()
                for k in range(8):
                    p.op(PE, lambda e, ps=ps, k=k, jj=jj, ws=ws: e.matmul(
                        ps[:, 0:TT], ws[:, k, jj * 128:(jj + 1) * 128], hbf[:, k, :], start=(k == 0), stop=(k == 7)),
                        reads=[ws[:, k, jj * 128:(jj + 1) * 128], hbf[:, k, :]], writes=[ps])
                uc = u32[:, c, HK:HK + TT]
                bj = bp1[:, j:j + 1]
                if j < 8:
                    p.op(ACT, lambda e, ps=ps, uc=uc, bj=bj: e.activation(out=uc, in_=ps[:, 0:TT], func=AF.Identity, bias=bj),
                         reads=[ps, bj], writes=[uc])
                else:
                    s_ = sg[j % 2]
                    p.op(ACT, lambda e, ps=ps, s_=s_, bj=bj: e.activation(out=s_, in_=ps[:, 0:TT], func=AF.Sigmoid, bias=bj),
                         reads=[ps, bj], writes=[s_])
                    p.op(DVE, lambda e, uc=uc, s_=s_: e.tensor_tensor(out=uc, in0=uc, in1=s_, op=ALU.mult), reads=[uc, s_], writes=[uc])
            self.slab_done(gi)
        p.op(DVE, lambda e: e.tensor_copy(out=self.uhist[:], in_=u32[:, :, TT:TT + HK]), reads=[u32[:, :, TT:TT + HK]], writes=[self.uhist[:]])
        p.op(ACT, lambda e: e.activation(out=ub, in_=u32, func=AF.Copy), reads=[u32], writes=[ub])
        v32 = A.alloc([8, TT], F32)
        dwb = self.vc("dwb", 8)
        macc = self.acc_new(self.mbank, 8, TT, False)
        vacc = self.acc_new(self.sbank, 8, TT, True)
        for c in range(8):
            d, mw, gi = self.next_slab("dgc")
            ps = self.bank()
            for k in range(KW):
                p.op(PE, lambda e, ps=ps, k=k, d=d, c=c: e.matmul(ps[:, 0:TT], d[:, k, :], ub[:, c, k:k + TT], start=(k == 0), stop=(k == KW - 1)),
                     reads=[d[:, k, :], ub[:, c, k:k + TT]], writes=[ps])
            self.slab_done(gi)
            p.op(ACT, lambda e, ps=ps, c=c: e.activation(out=v32[:, c, :], in_=ps[:, 0:TT], func=AF.Identity, bias=dwb[:, c:c + 1]),
                 reads=[ps, dwb[:, c:c + 1]], writes=[v32[:, c, :]])
            self.acc_push(macc, v32[:, c, :], c)
            self.acc_push(vacc, v32[:, c, :], c)
        mean = A.alloc([TT], F32)
        m2 = A.alloc([TT], F32)
        rstd = A.alloc([TT], F32)
        p.op(ACT, lambda e: e.activation(out=mean, in_=self.mbank[:, 0:TT], func=AF.Identity, scale=1.0 / D), reads=[self.mbank], writes=[mean])
        p.op(DVE, lambda e: e.tensor_tensor(out=m2, in0=mean, in1=mean, op=ALU.mult), reads=[mean], writes=[m2])
        p.op(DVE, lambda e: e.scalar_tensor_tensor(out=rstd, in0=self.sbank[:, 0:TT], scalar=1.0 / D, in1=m2, op0=ALU.mult, op1=ALU.subtract),
             reads=[self.sbank, m2], writes=[rstd])
        p.op(ACT, lambda e: e.activation(out=rstd, in_=rstd, func=AF.Sqrt, bias=EPS), reads=[rstd], writes=[rstd])
        p.op(DVE, lambda e: e.reciprocal(out=rstd, in_=rstd), reads=[rstd], writes=[rstd])
        hb2 = A.alloc([8, TT], BF16)
        lng, lnb = self.vc("lng", 8), self.vc("lnb", 8)
        for h0 in range(0, 8, 2):
            vq = v32[:, h0:h0 + 2, :]
            p.op(DVE, lambda e, vq=vq: e.tensor_tensor(out=vq, in0=vq, in1=bc(mean.unsqueeze(1), [128, 2, TT]), op=ALU.subtract),
                 reads=[vq, mean], writes=[vq])
            p.op(DVE, lambda e, vq=vq: e.tensor_tensor(out=vq, in0=vq, in1=bc(rstd.unsqueeze(1), [128, 2, TT]), op=ALU.mult),
                 reads=[vq, rstd], writes=[vq])
            for c in range(h0, h0 + 2):
                p.op(ACT, lambda e, c=c: e.activation(out=hb2[:, c, :], in_=v32[:, c, :], func=AF.Silu, bias=lnb[:, c:c + 1], scale=lng[:, c:c + 1]),
                     reads=[v32[:, c, :], lnb[:, c:c + 1], lng[:, c:c + 1]], writes=[hb2[:, c, :]])
        gt2 = self.modc[:, 1, 40:48, g]
        bg = self.der[:, g, 1, 5, :]
        tt = [A.alloc([TT], F32) for _ in range(2)]
        for sl in range(2):
            ws, mw, gi = self.next_slab("pw2")
            for mm in range(4):
                m = 4 * sl + mm
                ps = self.bank()
                for k in range(8):
                    p.op(PE, lambda e, ps=ps, k=k, mm=mm, ws=ws: e.matmul(
                        ps[:, 0:TT], ws[:, k, mm * 128:(mm + 1) * 128], hb2[:, k, :], start=(k == 0), stop=(k == 7)),
                        reads=[ws[:, k, mm * 128:(mm + 1) * 128], hb2[:, k, :]], writes=[ps])
                t_ = tt[m % 2]
                p.op(ACT, lambda e, ps=ps, t_=t_, m=m: e.activation(out=t_, in_=ps[:, 0:TT], func=AF.Identity,
                                                                    bias=bg[:, m:m + 1], scale=gt2[:, m:m + 1]),
                     reads=[ps, bg[:, m:m + 1], gt2[:, m:m + 1]], writes=[t_])
                xm = xres[:, m, :]
                p.op(DVE, lambda e, t_=t_, xm=xm: e.tensor_tensor(out=xm, in0=xm, in1=t_, op=ALU.add), reads=[xm, t_], writes=[xm])
                self.stat_push(TT, m)
            self.slab_done(gi)
        A.release(m0)

    def tile(self, src, dst, r0, TT, g, dbg=False, nxt=None, first=True):
        A = self.A
        if self.budget <= 0:
            return
        if first:
            self.load_x(src, r0, TT)
        stage = [0]
        steps = [lambda: self.ffn(TT, g, 0, 0, 0), lambda: self.ssd(TT, g), lambda: self.ffn(TT, g, 0, 2, 1),
                 lambda: self.ffn(TT, g, 1, 0, 0), lambda: self.convmod(TT, g),
                 lambda: ((self.prefetch_x(*nxt) if nxt else None), self.ffn(TT, g, 1, 2, 1))]

        def dump():
            if dbg and self.dbg:
                k = stage[0]
                self.p.dma(POOL, self.dbgo[k, :, :, 0:TT], self.xres[:, :, 0:TT], key="dbg")
                stage[0] += 1
        dump()
        for st in steps:
            if self.budget <= 0:
                return
            self.budget -= 1
            st()
            dump()
        if self.budget <= 0:
            return
        self.budget -= 1
        m0 = A.mark()
        yT = A.alloc([8, TT], F32)
        self.rms(self.xres[:, :, 0:TT], 8, TT, self.vc("fg", 8), None, yT, pre_stats=True,
                 mid=((lambda: self.load_x(*nxt)) if nxt else None))
        self.store_y(yT, dst, r0, TT)
        A.release(m0)

    def build(self, TT=512, budget=10 ** 9):
        S = self.S
        self.stgn = 0
        self.budget = budget
        self.setup()
        self.cast_weights()
        nt = S // TT
        self.start_wstream(nt + 1)
        self.init_state_zero()
        for t in range(nt):
            nxt = (self.xp, (t + 1) * TT, TT) if t + 1 < nt else None
            self.tile(self.xp, self.yp, t * TT, TT, 0, dbg=(t == 0), nxt=nxt, first=(t == 0))
        if self.budget > 0:
            self.store_states(0)
            self.budget -= 1
        if self.budget > 0:
            self.load_sample_state()
            self.budget -= 1
            self.enter_sample_mode()
        self.tile(self.xs, self.ys, 0, DEC, 1)
        if self.budget > 0:
            self.store_states(1)
        for q in (SP, ACT, POOL):
            self.p.barrier_all_dma(q)
        self.p.emit(self.nc, self.stack)
        self.stack.close()
        return self.nc


def make_in_maps(inputs, S=SEQ, ncores=NCORES):
    vecs, _ = _vec_rows(inputs)
    consts = _consts()
    shared = {
        "consts": consts, "vecs": vecs,
        "mod_w": np.ascontiguousarray(inputs["mod_w"], np.float32),
        "ffn_w1": np.ascontiguousarray(inputs["ffn_w1"], np.float32),
        "ffn_w3": np.ascontiguousarray(inputs["ffn_w3"], np.float32),
        "ffn_w2": np.ascontiguousarray(inputs["ffn_w2"], np.float32),
        "ssd_w_in": np.ascontiguousarray(inputs["ssd_w_in"], np.float32),
        "ssd_w_out": np.ascontiguousarray(inputs["ssd_w_out"], np.float32),
        "cmod_w_pw1": np.ascontiguousarray(inputs["cmod_w_pw1"], np.float32),
        "cmod_w_pw2": np.ascontiguousarray(inputs["cmod_w_pw2"], np.float32),
    }
    maps = []
    for b in range(ncores):
        m = dict(shared)
        m["xp"] = np.ascontiguousarray(inputs["x_prompt"][b, :S], np.float32)
        m["xs"] = np.ascontiguousarray(inputs["x_sample"][b], np.float32)
        m["c2"] = np.ascontiguousarray(np.stack([inputs["c_prompt"][b], inputs["c_sample"][b]]), np.float32)
        m["st_in"] = np.ascontiguousarray(inputs["state_ssd"][0, b], np.float32).reshape(NH * HP, NST)
        m["sc_in"] = np.ascontiguousarray(inputs["cache_ssd_conv"][0, b], np.float32)
        m["cc_in"] = np.ascontiguousarray(inputs["cache_cmod_conv"][0, b], np.float32)
        maps.append(m)
    return maps


def run(inputs, S=SEQ, ncores=NCORES, dbg=False, trace=False, budget=10 ** 9):
    inputs = {k: np.asarray(v) for k, v in inputs.items()}
    b = Builder(S, dbg=dbg)
    nc = b.build(budget=budget)
    maps = make_in_maps(inputs, S, ncores)
    kw = {"trace": True} if trace else {}
    res = run_bass_kernel_spmd(nc, maps, core_ids=list(range(ncores)), **kw)
    return res, b


def kernel(**inputs):
    res, _ = run(inputs)
    r = res.results
    n = NCORES
    yp = np.stack([r[b]["yp"] for b in range(n)]).astype(np.float32)
    ys = np.stack([r[b]["ys"] for b in range(n)]).astype(np.float32)

    def st(name):
        return np.stack([r[b][name].reshape(NH, HP, NST) for b in range(n)])[None].astype(np.float32)

    def plain(name):
        return np.stack([r[b][name] for b in range(n)])[None].astype(np.float32)

    return (yp, ys, st("st_p"), plain("sc_p"), plain("cc_p"), st("st_s"), plain("sc_s"), plain("cc_s"))
```
